# Optimizing a Trainium2 kernel written in Bass

```python
import jax, jax.numpy as jnp
from jax import lax
import numpy as np

D_MODEL = 2048
BATCH = 1
SEQ = 16384
DEPTH = 2

BRANCH_WIDTH = 512
N_BRANCHES = 4
GROUP_WIDTH = 128
N_GROUPS = BRANCH_WIDTH // GROUP_WIDTH
CONV_KERNEL = 31
POOL_WINDOWS = (2, 4, 8, 16)
SGU_CHUNK = 128
SHORT_CONV_KERNEL = 3
D_FF = 5632
N_SUBLAYERS = 3
N_MOD = 3
RMS_EPS = 1e-6
LN_EPS = 1e-5
MIX_SPLITS = (2 * BRANCH_WIDTH, 3 * BRANCH_WIDTH, 5 * BRANCH_WIDTH)
MIX_IN_WIDTH = 8 * BRANCH_WIDTH

kernel_name = "hybrid_gated_conv_pool_sgu_block"


def rms_norm(x, g):
    x32 = x.astype(jnp.float32)
    y = x32 * lax.rsqrt(jnp.mean(x32 * x32, axis=-1, keepdims=True) + RMS_EPS)
    return (y * g.astype(jnp.float32)).astype(x.dtype)


def layer_norm(x, g, b):
    x32 = x.astype(jnp.float32)
    mu = jnp.mean(x32, axis=-1, keepdims=True)
    xc = x32 - mu
    y = xc * lax.rsqrt(jnp.mean(xc * xc, axis=-1, keepdims=True) + LN_EPS)
    return (y * g.astype(jnp.float32) + b.astype(jnp.float32)).astype(x.dtype)


def modulate(n, shift, scale):
    return n * (1 + scale[:, None, :]) + shift[:, None, :]


def causal_depthwise_conv(x, w):
    k = w.shape[0]
    return lax.conv_general_dilated(
        x, w[:, None, :].astype(x.dtype), window_strides=(1,), padding=[(k - 1, 0)],
        dimension_numbers=("NWC", "WIO", "NWC"), feature_group_count=x.shape[-1])


def swiglu_ffn(n, w_in, w_out):
    gate, up = jnp.split(n @ w_in, 2, axis=-1)
    return (jax.nn.silu(gate) * up) @ w_out


def conformer_conv_mixer(p, conv_w, conv_b, ln_g, ln_b):
    val, gate = jnp.split(p, 2, axis=-1)
    a = val * jax.nn.sigmoid(gate)
    a = causal_depthwise_conv(a, conv_w) + conv_b
    return jax.nn.silu(layer_norm(a, ln_g, ln_b))


def multiscale_pool_mixer(p, group_w, scale):
    b, s, _ = p.shape
    xg = p.reshape(b, s, N_GROUPS, GROUP_WIDTH).astype(jnp.float32)
    pos1 = jnp.arange(1, s + 1, dtype=jnp.int32)
    outs = []
    for gi, win in enumerate(POOL_WINDOWS):
        seg = xg[:, :, gi]
        cs = jnp.cumsum(seg, axis=1)
        lag = jnp.pad(cs, ((0, 0), (win, 0), (0, 0)))[:, :s]
        cnt = jnp.minimum(pos1, win).astype(jnp.float32)[None, :, None]
        outs.append((cs - lag) / cnt - seg)
    pooled = jnp.stack(outs, axis=2).astype(p.dtype)
    mixed = jnp.einsum("bsgc,gcd->bsgd", pooled, group_w)
    return mixed.reshape(b, s, BRANCH_WIDTH) * scale


def spatial_gating_mixer(p, ln_g, ln_b, w_s, b_s):
    b, s, _ = p.shape
    u, v = jnp.split(jax.nn.gelu(p, approximate=False), 2, axis=-1)
    v = layer_norm(v, ln_g, ln_b).reshape(b, s // SGU_CHUNK, SGU_CHUNK, N_GROUPS, GROUP_WIDTH)
    mask = jnp.tril(jnp.ones((SGU_CHUNK, SGU_CHUNK), dtype=bool))
    w_causal = jnp.where(mask[None], w_s, 0)
    sv = jnp.einsum("gtj,bnjgc->bntgc", w_causal, v) + b_s.T[None, None, :, :, None]
    return u * sv.reshape(b, s, BRANCH_WIDTH)


def short_conv_mixer(p, conv_w):
    gb, gc, h = jnp.split(p, 3, axis=-1)
    return gb * causal_depthwise_conv(gc * h, conv_w)


def setup_inputs(seed: int = 0) -> dict:
    key = jax.random.key(seed)
    ks = jax.random.split(key, 24)
    L, D, W, F, G, T = DEPTH, D_MODEL, BRANCH_WIDTH, D_FF, N_GROUPS, SGU_CHUNK
    nrm = lambda k, shape, s: jax.random.normal(k, shape, jnp.float32) * s
    return {
        "x": nrm(ks[0], (BATCH, SEQ, D), 1.0),
        "c": nrm(ks[1], (BATCH, D), 1.0),
        "ada_w": nrm(ks[2], (L, D, N_SUBLAYERS * N_MOD * D), 0.5 * D ** -0.5),
        "ada_b": nrm(ks[3], (L, N_SUBLAYERS * N_MOD * D), 0.01),
        "pre_g": 1.0 + nrm(ks[4], (L, N_SUBLAYERS, D), 0.05),
        "post_g": 1.0 + nrm(ks[5], (L, N_SUBLAYERS, D), 0.05),
        "ffn_w_in": nrm(ks[6], (L, 2, D, 2 * F), D ** -0.5),
        "ffn_w_out": nrm(ks[7], (L, 2, F, D), F ** -0.5),
        "mix_w_in": nrm(ks[8], (L, D, MIX_IN_WIDTH), D ** -0.5),
        "gate_w": nrm(ks[9], (L, N_BRANCHES, D, D), D ** -0.5),
        "gate_b": nrm(ks[10], (L, N_BRANCHES, D), 0.01),
        "conv_w": nrm(ks[11], (L, CONV_KERNEL, W), CONV_KERNEL ** -0.5),
        "conv_b": nrm(ks[12], (L, W), 0.01),
        "conv_ln_g": 1.0 + nrm(ks[13], (L, W), 0.05),
        "conv_ln_b": nrm(ks[14], (L, W), 0.01),
        "pool_group_w": nrm(ks[15], (L, G, GROUP_WIDTH, GROUP_WIDTH), GROUP_WIDTH ** -0.5),
        "pool_scale": 1.0 + nrm(ks[16], (L, W), 0.05),
        "sgu_ln_g": 1.0 + nrm(ks[17], (L, W), 0.05),
        "sgu_ln_b": nrm(ks[18], (L, W), 0.01),
        "sgu_w_s": nrm(ks[19], (L, G, T, T), T ** -0.5),
        "sgu_b_s": 1.0 + nrm(ks[20], (L, G, T), 0.05),
        "sconv_w": nrm(ks[21], (L, SHORT_CONV_KERNEL, W), SHORT_CONV_KERNEL ** -0.5),
        "branch_w_out": nrm(ks[22], (L, N_BRANCHES, W, D), W ** -0.5),
        "w_o": nrm(ks[23], (L, D, D), D ** -0.5),
    }


def reference(x, c, ada_w, ada_b, pre_g, post_g, ffn_w_in, ffn_w_out, mix_w_in, gate_w, gate_b,
              conv_w, conv_b, conv_ln_g, conv_ln_b, pool_group_w, pool_scale, sgu_ln_g, sgu_ln_b,
              sgu_w_s, sgu_b_s, sconv_w, branch_w_out, w_o):
    b = x.shape[0]
    cond = jax.nn.silu(c)
    for l in range(DEPTH):
        ada = (cond @ ada_w[l] + ada_b[l]).reshape(b, N_SUBLAYERS, N_MOD, D_MODEL)

        n = modulate(rms_norm(x, pre_g[l, 0]), ada[:, 0, 0], ada[:, 0, 1])
        y = rms_norm(swiglu_ffn(n, ffn_w_in[l, 0], ffn_w_out[l, 0]), post_g[l, 0])
        x = x + 0.5 * ada[:, 0, 2][:, None, :] * y

        n = modulate(rms_norm(x, pre_g[l, 1]), ada[:, 1, 0], ada[:, 1, 1])
        p_conv, p_pool, p_sgu, p_sconv = jnp.split(n @ mix_w_in[l], MIX_SPLITS, axis=-1)
        y_conv = conformer_conv_mixer(p_conv, conv_w[l], conv_b[l], conv_ln_g[l], conv_ln_b[l])
        y_pool = multiscale_pool_mixer(p_pool, pool_group_w[l], pool_scale[l])
        y_sgu = spatial_gating_mixer(p_sgu, sgu_ln_g[l], sgu_ln_b[l], sgu_w_s[l], sgu_b_s[l])
        y_sconv = short_conv_mixer(p_sconv, sconv_w[l])
        merged = (jax.nn.sigmoid(n @ gate_w[l, 0] + gate_b[l, 0]) * (y_conv @ branch_w_out[l, 0])
                  + jax.nn.sigmoid(n @ gate_w[l, 1] + gate_b[l, 1]) * (y_pool @ branch_w_out[l, 1])
                  + jax.nn.sigmoid(n @ gate_w[l, 2] + gate_b[l, 2]) * (y_sgu @ branch_w_out[l, 2])
                  + jax.nn.sigmoid(n @ gate_w[l, 3] + gate_b[l, 3]) * (y_sconv @ branch_w_out[l, 3]))
        y = rms_norm(merged @ w_o[l], post_g[l, 1])
        x = x + ada[:, 1, 2][:, None, :] * y

        n = modulate(rms_norm(x, pre_g[l, 2]), ada[:, 2, 0], ada[:, 2, 1])
        y = rms_norm(swiglu_ffn(n, ffn_w_in[l, 1], ffn_w_out[l, 1]), post_g[l, 2])
        x = x + 0.5 * ada[:, 2, 2][:, None, :] * y
    return x
```

```python
import numpy as np
import concourse.bass as bass
import concourse.mybir as mybir
from concourse.bass_utils import run_bass_kernel_spmd

F32 = mybir.dt.float32
BF16 = mybir.dt.bfloat16
AF = mybir.ActivationFunctionType
ALU = mybir.AluOpType
AX = mybir.AxisListType

ENGS = ("tensor", "scalar", "vector", "gpsimd", "sync")

D = 2048
FF = 5632
BW = 512
NL = 2
NCORES = 8
TOK_PER_CORE = 2048
HALO = 128
TMAX = 384
RMS_EPS = 1e-6
LN_EPS = 1e-5
SLOT = 8192
NSLOT = 3


class V:
    __slots__ = ("ap", "key")

    def __init__(self, ap, key):
        self.ap = ap
        self.key = key


class Plan:
    def __init__(self, nc):
        self.nc = nc
        self.ops = {e: [] for e in ENGS}
        self.semname = {}
        self.cnt = {}
        self.seen = {e: {} for e in ENGS}
        self.hist = {}
        self.recs = {}
        self.handles = {}
        self.nsem = 0
        for e in ENGS:
            self._new_eng_sem(e)

    def new_sem(self, name):
        h = self.nc.alloc_semaphore(name)
        self.handles[name] = h
        self.cnt[name] = 0
        self.nsem += 1
        return name

    def _new_eng_sem(self, e):
        name = f"p_{e}_{self.nsem}"
        self.new_sem(name)
        self.semname[e] = name

    def new_epoch(self):
        for e in ENGS:
            self._new_eng_sem(e)

    def _ov(self, key):
        name, lo, hi = key
        lst = self.recs.setdefault(name, [])
        return lst, [r for r in lst if r[0] < hi and lo < r[1]]

    def _deps_for(self, reads, writes):
        deps = {}

        def add(s, v):
            if deps.get(s, 0) < v:
                deps[s] = v

        for k in reads:
            for r in self._ov(k)[1]:
                if r[2] is not None:
                    add(*r[2])
        for k in writes:
            for r in self._ov(k)[1]:
                if r[2] is not None:
                    add(*r[2])
                for s, v in r[3].items():
                    add(s, v)
        return deps

    def _record(self, reads, writes, tok):
        s, v = tok
        for k in reads:
            lst, ov = self._ov(k)
            exact = None
            for r in ov:
                if r[0] == k[1] and r[1] == k[2]:
                    exact = r
            if exact is None:
                exact = [k[1], k[2], None, {}]
                lst.append(exact)
            if exact[3].get(s, 0) < v:
                exact[3][s] = v
        for k in writes:
            lst, ov = self._ov(k)
            for r in ov:
                if k[1] <= r[0] and r[1] <= k[2]:
                    lst.remove(r)
            lst.append([k[1], k[2], (s, v), {}])

    def _waits(self, eng, deps, skip_self=False):
        waits = []
        seen = self.seen[eng]
        own = self.semname[eng]
        for s, v in deps.items():
            if skip_self and s == own:
                continue
            if seen.get(s, 0) >= v:
                if s != own:
                    continue
                if seen.get(("own", s), 0) >= v:
                    continue
            waits.append((s, v))
            if s == own:
                seen[("own", s)] = v
            if seen.get(s, 0) < v:
                seen[s] = v
            h = self.hist.get((s, v))
            if h:
                for s2, v2 in h.items():
                    if seen.get(s2, 0) < v2:
                        seen[s2] = v2
        return waits

    @staticmethod
    def _keys(lst):
        return [x.key if isinstance(x, V) else x for x in lst]

    def op(self, eng, fn, reads=(), writes=()):
        rk = self._keys(reads)
        wk = self._keys(writes)
        wk = wk + [k for k in rk if k[0].startswith("ps")]
        rk = [k for k in rk if not k[0].startswith("ps")]
        deps = self._deps_for(rk, wk)
        waits = self._waits(eng, deps, skip_self=(eng == "tensor"))
        s = self.semname[eng]
        self.cnt[s] += 1
        v = self.cnt[s]
        self.seen[eng][s] = v
        self.hist[(s, v)] = {k: x for k, x in self.seen[eng].items() if not isinstance(k, tuple)}
        self.ops[eng].append((waits, fn, (s, 1)))
        self._record(rk, wk, (s, v))
        return (s, v)

    def dma(self, eng, pairs, dsem, reads=(), writes=()):
        rk = self._keys(reads)
        wk = self._keys(writes)
        deps = self._deps_for(rk, wk)
        if self.cnt[dsem] > 0 and deps.get(dsem, 0) < self.cnt[dsem]:
            deps[dsem] = self.cnt[dsem]
        waits = self._waits(eng, deps)
        self.cnt[dsem] += 16 * len(pairs)
        v = self.cnt[dsem]
        self.hist[(dsem, v)] = {k: x for k, x in self.seen[eng].items() if not isinstance(k, tuple)}
        h = self.handles[dsem]

        def fn(e, pairs=pairs, h=h):
            for (o, i) in pairs:
                e.dma_start(out=o, in_=i).then_inc(h, 16)
            return None

        self.ops[eng].append((waits, fn, None))
        self._record(rk, wk, (dsem, v))
        return (dsem, v)

    def wait_on(self, eng, tok):
        waits = self._waits(eng, {tok[0]: tok[1]})
        if waits:
            self.ops[eng].append((waits, None, None))

    def emit(self):
        nc = self.nc
        handles = self.handles
        ops = self.ops
        with nc.Block() as block:
            def run(e, lst):
                for waits, fn, inc in lst:
                    for s, v in waits:
                        e.wait_ge(handles[s], v)
                    if fn is None:
                        continue
                    ins = fn(e)
                    if inc is not None:
                        ins.then_inc(handles[inc[0]], inc[1])

            @block.tensor
            def _(e):
                run(e, ops["tensor"])

            @block.scalar
            def _(e):
                run(e, ops["scalar"])

            @block.vector
            def _(e):
                run(e, ops["vector"])

            @block.gpsimd
            def _(e):
                run(e, ops["gpsimd"])

            @block.sync
            def _(e):
                run(e, ops["sync"])


class Tn:
    def __init__(self, nc, name, cols, dtype, psum=False, parts=128):
        self.name = name
        self.cols = cols
        if psum:
            self.t = nc.alloc_psum_tensor(name, [parts, cols], dtype)
        else:
            self.t = nc.alloc_sbuf_tensor("sb_" + name, [parts, cols], dtype)

    def v(self, lo, hi):
        return V(self.t[:, lo:hi], (self.name, lo, hi))

    def v3(self, lo, a, b, sub=None):
        ap = self.t[:, lo:lo + a * b].rearrange("p (a b) -> p a b", a=a)
        if sub is not None:
            ap = ap[:, :, sub[0]:sub[1]]
            return V(ap, (self.name, lo + sub[0], lo + (a - 1) * b + sub[1]))
        return V(ap, (self.name, lo, lo + a * b))


def _cpp_layout():
    off = {}
    n = 0

    def add(name, cols):
        nonlocal n
        off[name] = n
        n += cols

    add("c", 16)
    add("ada_b", NL * 144)
    add("pre_g", NL * 3 * 16)
    add("post_g", NL * 3 * 16)
    add("gate_b", NL * 4 * 16)
    add("conv_w", NL * 4 * 31)
    add("conv_b", NL * 4)
    add("conv_ln_g", NL * 4)
    add("conv_ln_b", NL * 4)
    add("pool_scale", NL * 4)
    add("sgu_ln_g", NL * 4)
    add("sgu_ln_b", NL * 4)
    add("sconv_w", NL * 4 * 3)
    add("mask", 1)
    add("corr", 4 * 16)
    return off, n


CPP_OFF, CPP_N = _cpp_layout()


def _pp(a):
    a = np.asarray(a, dtype=np.float32)
    lead = a.shape[:-1]
    n = a.shape[-1] // 128
    a = a.reshape(lead + (n, 128))
    a = np.moveaxis(a, -1, 0)
    return np.ascontiguousarray(a.reshape(128, -1))


def _build_cpp(inp, core):
    cpp = np.zeros((128, CPP_N), np.float32)

    def put(name, arr):
        cpp[:, CPP_OFF[name]:CPP_OFF[name] + arr.shape[1]] = arr

    put("c", _pp(inp["c"][0]))
    put("ada_b", _pp(inp["ada_b"]))
    put("pre_g", _pp(inp["pre_g"]))
    put("post_g", _pp(inp["post_g"]))
    put("gate_b", _pp(inp["gate_b"]))
    cw = np.asarray(inp["conv_w"], np.float32).reshape(NL, 31, 4, 128)
    put("conv_w", np.ascontiguousarray(cw.transpose(3, 0, 2, 1).reshape(128, -1)))
    for nm in ("conv_b", "conv_ln_g", "conv_ln_b", "pool_scale", "sgu_ln_g", "sgu_ln_b"):
        put(nm, _pp(inp[nm]))
    sw = np.asarray(inp["sconv_w"], np.float32).reshape(NL, 3, 4, 128)
    put("sconv_w", np.ascontiguousarray(sw.transpose(3, 0, 2, 1).reshape(128, -1)))
    cpp[:, CPP_OFF["mask"]] = 0.0 if core == 0 else 1.0
    corr = np.ones((4, 16), np.float32)
    if core == 0:
        for g, w in enumerate((2, 4, 8, 16)):
            for t in range(16):
                corr[g, t] = float(w) / float(min(t + 1, w))
    cpp[:, CPP_OFF["corr"]:CPP_OFF["corr"] + 64] = corr.reshape(1, 64)
    return cpp


def build(blocks, n_layers=NL, stages=("ffn0", "mix", "ffn1"), scratch=True, debug=False):
    ntok = sum(blocks)
    nc = bass.Bass("TRN2", target_bir_lowering=False)
    P = Plan(nc)

    def din(name, shape, dt=F32):
        return nc.dram_tensor(name, list(shape), dt, kind="ExternalInput").ap()

    x_d = din("x", [ntok, D])
    out_d = nc.dram_tensor("out", [ntok - HALO, D], F32, kind="ExternalOutput").ap()
    cpp_d = din("cpp", [128, CPP_N])
    bsb_d = din("bsb", [128, NL * 512])
    ident_d = din("ident", [128, 128])
    cmask_d = din("cmask", [128, 128])
    ada_w_d = din("ada_w", [NL, D, 18432])
    win_d = din("ffn_w_in", [NL, 2, D, 2 * FF])
    wout_d = din("ffn_w_out", [NL, 2, FF, D])
    mixw_d = din("mix_w_in", [NL, D, 4096])
    gatew_d = din("gate_w", [NL, 4, D, D])
    brw_d = din("branch_w_out", [NL, 4, BW, D])
    wo_d = din("w_o", [NL, D, D])
    poolw_d = din("pool_group_w", [NL, 4, 128, 128])
    sguw_d = din("sgu_w_sT", [NL, 4, 128, 128])

    if debug:
        dbg_d = nc.dram_tensor("dbg", [128, 16 * TMAX], BF16, kind="ExternalOutput").ap()
        dbg2_d = nc.dram_tensor("dbg2", [128, 16 * TMAX], BF16, kind="ExternalOutput").ap()
    TILES_PER_LAYER = 100
    if scratch:
        wsc_l = [nc.dram_tensor(f"wscratch{l}", [TILES_PER_LAYER, 128, SLOT], BF16, kind="Internal").ap() for l in range(NL)]
        wsc_d = {l * TILES_PER_LAYER + i: wsc_l[l][i] for l in range(NL) for i in range(TILES_PER_LAYER)}

    TM = TMAX
    xT = Tn(nc, "xT", 16 * TM, F32)
    MS = Tn(nc, "MS", 12288, F32)
    nT = Tn(nc, "nT", 16 * TM, BF16)
    hT = Tn(nc, "hT", 44 * TM, BF16)
    wring = [Tn(nc, f"w{i}", SLOT, BF16) for i in range(NSLOT)]
    wsem = [P.new_sem(f"ws{i}") for i in range(NSLOT)]
    wbsem = [P.new_sem(f"wb{i}") for i in range(NSLOT)]
    cpp = Tn(nc, "cpp", CPP_N, F32)
    bsb = Tn(nc, "bsb", NL * 512, F32)
    ident = Tn(nc, "ident", 128, F32)
    identb = Tn(nc, "identb", 128, BF16)
    ones = Tn(nc, "ones", 128, BF16)
    cmask = Tn(nc, "cmask", 128, F32)
    poolw = Tn(nc, "poolw", NL * 512, BF16)
    sguw = Tn(nc, "sguw", NL * 512, BF16)
    adar = Tn(nc, "adar", NL * 144, F32)
    cond = Tn(nc, "cond", 16, F32)
    prm = Tn(nc, "prm", 3 * NL * 48, F32)
    TP = Tn(nc, "TP", 8 * TM, F32)
    cst = Tn(nc, "cst", NL * 4 * 48, F32)
    NPS = 7
    ps = [Tn(nc, f"ps{i}", 512, F32, psum=True) for i in range(NPS)]
    psb = Tn(nc, "psb", 1024, BF16, psum=True)
    bank_i = [0]

    def bank():
        b = ps[bank_i[0] % NPS]
        bank_i[0] += 1
        return b

    dsem = {k: P.new_sem(k) for k in ("ld", "xin", "xout")}

    def cc(name, idx, n=1):
        o = CPP_OFF[name] + idx
        return cpp.v(o, o + n)

    def xc(c, T):
        return xT.v(c * TM, c * TM + T)

    def nc_(c, T):
        return nT.v(c * TM, c * TM + T)

    def hc(c, T):
        return hT.v(c * TM, c * TM + T)

    def yc(c, T):
        return MS.v(c * TM, c * TM + T)

    def tp(i, T):
        return TP.v(i * TM, i * TM + T)

    P.dma("sync", [(cpp.t[:], cpp_d), (bsb.t[:], bsb_d), (ident.t[:], ident_d), (cmask.t[:], cmask_d)], dsem["ld"],
          writes=[cpp.v(0, CPP_N), bsb.v(0, NL * 512), ident.v(0, 128), cmask.v(0, 128)])
    P.op("vector", lambda e: e.tensor_copy(out=identb.t[:], in_=ident.t[:]), reads=[ident.v(0, 128)], writes=[identb.v(0, 128)])
    P.op("vector", lambda e: e.memset(ones.t[:], 1.0), writes=[ones.v(0, 128)])
    P.op("vector", lambda e: e.memset(cst.t[:], 0.0), writes=[cst.v(0, NL * 4 * 48)])
    for l in range(n_layers):
        stg = MS.v(0, 1024)
        P.dma("sync", [(MS.t[:, 0:512].rearrange("p (g d) -> p g d", g=4), poolw_d[l].rearrange("g c d -> c g d")),
                       (MS.t[:, 512:1024].rearrange("p (g t) -> p g t", g=4), sguw_d[l].rearrange("g j t -> j g t"))],
              dsem["ld"], writes=[stg])
        P.op("vector", lambda e, l=l: e.tensor_copy(out=poolw.t[:, l * 512:(l + 1) * 512], in_=MS.t[:, 0:512]),
             reads=[MS.v(0, 512)], writes=[poolw.v(l * 512, (l + 1) * 512)])
        P.op("vector", lambda e, l=l: e.tensor_tensor(
            out=sguw.t[:, l * 512:(l + 1) * 512].rearrange("p (g t) -> p g t", g=4),
            in0=MS.t[:, 512:1024].rearrange("p (g t) -> p g t", g=4),
            in1=cmask.t[:, 0:128].unsqueeze(1).broadcast_to([128, 4, 128]), op=ALU.mult),
            reads=[MS.v(512, 1024), cmask.v(0, 128)], writes=[sguw.v(l * 512, (l + 1) * 512)])

    wstate = {"i": 0, "converted": set(), "pinned": set(), "blk": 0, "br": 0}

    def getw(tile_id, pairs_fn, pin=False):
        si = wstate["i"] % NSLOT
        wstate["i"] += 1
        while si in wstate["pinned"]:
            si = wstate["i"] % NSLOT
            wstate["i"] += 1
        if pin:
            wstate["pinned"].add(si)
        slot = wring[si]
        full = slot.v(0, SLOT)
        if scratch and tile_id is not None and tile_id in wstate["converted"]:
            P.dma("sync", [(slot.t[:], wsc_d[tile_id])], wsem[si], reads=[("wsc", tile_id, tile_id + 1)], writes=[full])
        else:
            P.dma("gpsimd", pairs_fn(slot.t), wsem[si], writes=[full])
            if scratch and tile_id is not None:
                P.dma("sync", [(wsc_d[tile_id], slot.t[:])], wbsem[si], reads=[full], writes=[("wsc", tile_id, tile_id + 1)])
                wstate["converted"].add(tile_id)
        return slot

    brbuf = [Tn(nc, f"brb{i}", 2048, BF16) for i in range(2)]
    brsem = [P.new_sem(f"brs{i}") for i in range(2)]
    brwb = [P.new_sem(f"brw{i}") for i in range(2)]

    def getbr(l, tile_id, qd, b):
        k = wstate["br"] % 2
        wstate["br"] += 1
        buf = brbuf[k]
        full = buf.v(0, 2048)
        key = ("wscb", tile_id * 4 + b, tile_id * 4 + b + 1)
        img = wsc_d[tile_id][:, b * 2048:(b + 1) * 2048] if scratch else None
        if scratch and (tile_id, b) in wstate["converted"]:
            P.dma("sync", [(buf.t[:], img)], brsem[k], reads=[key], writes=[full])
        else:
            P.dma("gpsimd", [(buf.t[:, 0:2048].rearrange("p (c d) -> p c d", c=4),
                              brw_d[l, b][:, qd * 512:(qd + 1) * 512].rearrange("(c p) d -> p c d", p=128))], brsem[k], writes=[full])
            if scratch:
                P.dma("sync", [(img, buf.t[:])], brwb[k], reads=[full], writes=[key])
                wstate["converted"].add((tile_id, b))
        return buf

    def kc_view(dram2d, c0, ncols):
        return dram2d.rearrange("(kc p) f -> p kc f", p=128)[:, :, c0:c0 + ncols]

    def slot3(slot_t, a, b):
        return slot_t[:, 0:a * b].rearrange("p (a b) -> p a b", a=a)

    def prologue_ada():
        P.op("scalar", lambda e: e.activation(out=cond.t[:], in_=cpp.t[:, CPP_OFF["c"]:CPP_OFF["c"] + 16], func=AF.Silu),
             reads=[cc("c", 0, 16)], writes=[cond.v(0, 16)])
        crep = hT.v(0, 2048)
        P.op("vector", lambda e: e.tensor_copy(out=hT.t[:, 0:2048].rearrange("p (k m) -> p k m", k=16),
                                               in_=cond.t[:, 0:16].unsqueeze(2).broadcast_to([128, 16, 128])),
             reads=[cond.v(0, 16)], writes=[crep])
        for l in range(n_layers):
            for n in range(36):
                slot = getw(None, lambda st, l=l, n=n: [(slot3(st, 16, 512), kc_view(ada_w_d[l], n * 512, 512))])
                bk = bank()

                def fn(e, slot=slot, bk=bk):
                    ins = None
                    for kc in range(16):
                        ins = e.matmul(bk.t[:, 0:512], lhsT=hT.t[:, kc * 128:(kc + 1) * 128],
                                       rhs=slot.t[:, kc * 512:(kc + 1) * 512], start=(kc == 0), stop=(kc == 15))
                    return ins
                P.op("tensor", fn, reads=[slot.v(0, SLOT), crep], writes=[bk.v(0, 512)])
                tmp = MS.v(2048, 2560)
                P.op("vector", lambda e, bk=bk: e.tensor_tensor(
                    out=MS.t[:, 2048:2560].rearrange("p (j q) -> p j q", j=4),
                    in0=bk.t[:, 0:512].rearrange("p (j q) -> p j q", j=4),
                    in1=ident.t[:, 0:128].unsqueeze(1).broadcast_to([128, 4, 128]), op=ALU.mult),
                    reads=[bk.v(0, 512), ident.v(0, 128)], writes=[tmp])
                o = l * 144 + n * 4
                P.op("vector", lambda e, o=o: e.tensor_reduce(
                    out=adar.t[:, o:o + 4], in_=MS.t[:, 2048:2560].rearrange("p (j q) -> p j q", j=4),
                    axis=AX.X, op=ALU.add),
                    reads=[tmp], writes=[adar.v(o, o + 4)])
        nA = n_layers * 144
        P.op("vector", lambda e: e.tensor_tensor(out=adar.t[:, 0:nA], in0=adar.t[:, 0:nA],
                                                 in1=cpp.t[:, CPP_OFF["ada_b"]:CPP_OFF["ada_b"] + nA], op=ALU.add),
             reads=[adar.v(0, nA), cc("ada_b", 0, nA)], writes=[adar.v(0, nA)])
        for l in range(n_layers):
            for s in range(3):
                base = l * 144 + s * 48
                po = (l * 3 + s) * 16
                A0, S0, B0 = po, NL * 48 + po, 2 * NL * 48 + po
                P.op("vector", lambda e, base=base, po=po, A0=A0: e.scalar_tensor_tensor(
                    out=prm.t[:, A0:A0 + 16], in0=adar.t[:, base + 16:base + 32], scalar=1.0,
                    in1=cpp.t[:, CPP_OFF["pre_g"] + po:CPP_OFF["pre_g"] + po + 16], op0=ALU.add, op1=ALU.mult),
                    reads=[adar.v(base, base + 48), cc("pre_g", po, 16)], writes=[prm.v(A0, A0 + 16)])
                P.op("vector", lambda e, base=base, S0=S0: e.tensor_copy(out=prm.t[:, S0:S0 + 16], in_=adar.t[:, base:base + 16]),
                     reads=[adar.v(base, base + 48)], writes=[prm.v(S0, S0 + 16)])
                P.op("vector", lambda e, base=base, po=po, B0=B0, s=s: e.scalar_tensor_tensor(
                    out=prm.t[:, B0:B0 + 16], in0=adar.t[:, base + 32:base + 48], scalar=(1.0 if s == 1 else 0.5),
                    in1=cpp.t[:, CPP_OFF["post_g"] + po:CPP_OFF["post_g"] + po + 16], op0=ALU.mult, op1=ALU.mult),
                    reads=[adar.v(base, base + 48), cc("post_g", po, 16)], writes=[prm.v(B0, B0 + 16)])

    def prmv(kind, l, s, dc):
        o = kind * NL * 48 + (l * 3 + s) * 16 + dc
        return prm.v(o, o + 1)

    def mm_group(bk, T, pairs, reads, col0=0, fine=None):
        if fine is not None:
            n = len(pairs)
            tok = None
            for i, (l_, r_) in enumerate(pairs):
                def fn1(e, l_=l_, r_=r_, i=i, n=n):
                    return e.matmul(bk.t[:, col0:col0 + T], lhsT=l_, rhs=r_, start=(i == 0), stop=(i == n - 1))
                tok = P.op("tensor", fn1, reads=fine[i], writes=[bk.v(0, 512)])
            return tok

        def fn(e, pairs=pairs, bk=bk):
            ins = None
            n = len(pairs)
            for i, (l_, r_) in enumerate(pairs):
                ins = e.matmul(bk.t[:, col0:col0 + T], lhsT=l_, rhs=r_, start=(i == 0), stop=(i == n - 1))
            return ins
        return P.op("tensor", fn, reads=reads, writes=[bk.v(0, 512)])

    def stats_sum(src_chunks, T):
        bk = bank()
        mm_group(bk, T, [(ones.t[:, 0:128], s.ap) for s in src_chunks], reads=None,
                 fine=[[ones.v(0, 128), s] for s in src_chunks])
        return bk

    def rstd_from(bk, T, scale, eps, out_v, pre_sub=None):
        if pre_sub is None:
            P.op("scalar", lambda e: e.activation(out=out_v.ap, in_=bk.t[:, 0:T], func=AF.Ln, bias=float(eps), scale=float(scale)),
                 reads=[bk.v(0, 512)], writes=[out_v])
        else:
            P.op("vector", lambda e: e.scalar_tensor_tensor(out=out_v.ap, in0=bk.t[:, 0:T], scalar=float(scale), in1=pre_sub.ap,
                                                            op0=ALU.mult, op1=ALU.subtract),
                 reads=[bk.v(0, 512), pre_sub], writes=[out_v])
            P.op("scalar", lambda e: e.activation(out=out_v.ap, in_=out_v.ap, func=AF.Ln, bias=float(eps), scale=1.0),
                 reads=[out_v], writes=[out_v])
        P.op("scalar", lambda e: e.activation(out=out_v.ap, in_=out_v.ap, func=AF.Exp, scale=-0.5), reads=[out_v], writes=[out_v])

    def prenorm(l, s, T):
        sq = [hc(c, T) for c in range(16)]
        for c in range(16):
            P.op("scalar", lambda e, c=c: e.activation(out=sq[c].ap, in_=xc(c, T).ap, func=AF.Square),
                 reads=[xc(c, T)], writes=[sq[c]])
        bk = stats_sum(sq, T)
        rstd = tp(0, T)
        rstd_from(bk, T, 1.0 / D, RMS_EPS, rstd)
        for c in range(16):
            t = tp(1 + (c % 3), T)
            P.op("vector", lambda e, c=c, t=t: e.scalar_tensor_tensor(out=t.ap, in0=xc(c, T).ap, scalar=prmv(0, l, s, c).ap,
                                                                    in1=rstd.ap, op0=ALU.mult, op1=ALU.mult),
                 reads=[xc(c, T), prmv(0, l, s, c), rstd], writes=[t])
            P.op("scalar", lambda e, c=c, t=t: e.activation(out=nc_(c, T).ap, in_=t.ap, func=AF.Identity,
                                                            bias=prmv(1, l, s, c).ap, scale=1.0),
                 reads=[t, prmv(1, l, s, c)], writes=[nc_(c, T)])

    def postnorm(l, s, T):
        sq = [nc_(c, T) for c in range(16)]
        bk = stats_sum(sq, T)
        rstd = tp(0, T)
        rstd_from(bk, T, 1.0 / D, RMS_EPS, rstd)
        for c in range(16):
            t = tp(1 + (c % 3), T)
            P.op("vector", lambda e, c=c, t=t: e.scalar_tensor_tensor(out=t.ap, in0=yc(c, T).ap, scalar=prmv(2, l, s, c).ap,
                                                                    in1=rstd.ap, op0=ALU.mult, op1=ALU.mult),
                 reads=[yc(c, T), prmv(2, l, s, c), rstd], writes=[t])
            P.op("vector", lambda e, c=c, t=t: e.tensor_tensor(out=xc(c, T).ap, in0=t.ap, in1=xc(c, T).ap, op=ALU.add),
                 reads=[t, xc(c, T)], writes=[xc(c, T)])

    def evac_y(bk, dc, T):
        P.op("vector", lambda e: e.tensor_copy(out=yc(dc, T).ap, in_=bk.t[:, 0:T]), reads=[bk.v(0, 512)], writes=[yc(dc, T)])
        P.op("scalar", lambda e: e.activation(out=nc_(dc, T).ap, in_=bk.t[:, 0:T], func=AF.Square),
             reads=[bk.v(0, 512)], writes=[nc_(dc, T)])

    def tile_base(l):
        return l * TILES_PER_LAYER

    def ffn(l, s, f, T):
        prenorm(l, s, T)
        tb = tile_base(l) + (0 if f == 0 else 34)
        nall = [nc_(c, T) for c in range(16)]
        for g in range(22):
            slot = getw(tb + g, lambda st, g=g: [
                (slot3(st, 16, 512)[:, :, 0:256], kc_view(win_d[l, f], g * 256, 256)),
                (slot3(st, 16, 512)[:, :, 256:512], kc_view(win_d[l, f], FF + g * 256, 256))])
            for h in range(2):
                j = 2 * g + h
                bg, bu = bank(), bank()
                mm_group(bg, T, [(slot.t[:, kc * 512 + h * 128:kc * 512 + h * 128 + 128], nall[kc].ap) for kc in range(16)],
                         reads=[slot.v(0, SLOT)] + nall,
                         fine=([[slot.v(0, SLOT), nall[kc]] for kc in range(16)] if (g == 0 and h == 0) else None))
                mm_group(bu, T, [(slot.t[:, kc * 512 + 256 + h * 128:kc * 512 + 256 + h * 128 + 128], nall[kc].ap) for kc in range(16)],
                         reads=[slot.v(0, SLOT)] + nall)
                st_ = tp(4 + (j % 3), T)
                P.op("scalar", lambda e, bg=bg, st_=st_: e.activation(out=st_.ap, in_=bg.t[:, 0:T], func=AF.Silu),
                     reads=[bg.v(0, 512)], writes=[st_])
                P.op("vector", lambda e, bu=bu, st_=st_, j=j: e.tensor_tensor(out=hc(j, T).ap, in0=st_.ap, in1=bu.t[:, 0:T], op=ALU.mult),
                     reads=[bu.v(0, 512), st_], writes=[hc(j, T)])
        jt_sizes = (16, 16, 12)
        for q in range(4):
            bks = [bank() for _ in range(4)]
            j0 = 0
            for jt, nj in enumerate(jt_sizes):
                slot = getw(tb + 22 + q * 3 + jt, lambda st, q=q, j0=j0, nj=nj: [
                    (slot3(st, nj, 512), wout_d[l, f][j0 * 128:(j0 + nj) * 128, q * 512:(q + 1) * 512].rearrange("(j p) c -> p j c", p=128))])

                def fn(e, slot=slot, bks=bks, j0=j0, nj=nj):
                    ins = None
                    for jj in range(nj):
                        j = j0 + jj
                        for i in range(4):
                            ins = e.matmul(bks[i].t[:, 0:T], lhsT=slot.t[:, jj * 512 + i * 128:jj * 512 + i * 128 + 128],
                                           rhs=hT.t[:, j * TM:j * TM + T], start=(j == 0), stop=(j == 43))
                    return ins
                P.op("tensor", fn, reads=[slot.v(0, SLOT)] + [hc(j0 + jj, T) for jj in range(nj)], writes=[b.v(0, 512) for b in bks])
                j0 += nj
            for i in range(4):
                evac_y(bks[i], 4 * q + i, T)
        postnorm(l, s, T)

    A_BUF, ACC, PBUF, PT0, PT1, UB, VB, GB, CH = 0, 1656, 3192, 4792, 5192, 5592, 7128, 8664, 10200
    ACC2 = PBUF

    def mixer(l, T, blk):
        NT = T // 128
        prenorm(l, 1, T)
        tb = tile_base(l) + 68
        nall = [nc_(c, T) for c in range(16)]
        AW, PW, CW = 30 + TM, 16 + TM, 2 + TM

        def abuf(c, lo, hi):
            return MS.v(A_BUF + c * AW + lo, A_BUF + c * AW + hi)

        def pbuf(c, lo, hi):
            return MS.v(PBUF + c * PW + lo, PBUF + c * PW + hi)

        def chb(c, lo, hi):
            return MS.v(CH + c * CW + lo, CH + c * CW + hi)

        def acc(c):
            return MS.v(ACC + c * TM, ACC + c * TM + T)

        def ub(c):
            return MS.v(UB + c * TM, UB + c * TM + T)

        def vb(c):
            return MS.v(VB + c * TM, VB + c * TM + T)

        def gb(c):
            return MS.v(GB + c * TM, GB + c * TM + T)

        def cstv(c, o, n):
            b = (l * 4 + c) * 48 + o
            return cst.v(b, b + n)

        for c in range(4):
            P.op("vector", lambda e, c=c: e.tensor_copy(out=abuf(c, 0, 30).ap, in_=cstv(c, 0, 30).ap), reads=[cstv(c, 0, 30)], writes=[abuf(c, 0, 30)])
            P.op("vector", lambda e, c=c: e.tensor_copy(out=pbuf(c, 0, 16).ap, in_=cstv(c, 30, 16).ap), reads=[cstv(c, 30, 16)], writes=[pbuf(c, 0, 16)])
            P.op("vector", lambda e, c=c: e.tensor_copy(out=chb(c, 0, 2).ap, in_=cstv(c, 46, 2).ap), reads=[cstv(c, 46, 2)], writes=[chb(c, 0, 2)])

        def proj_tile(mt):
            slot = getw(tb + mt, lambda st, mt=mt: [(slot3(st, 16, 512), kc_view(mixw_d[l], mt * 512, 512))])
            for i in range(4):
                c = i
                bk = bank()
                mm_group(bk, T, [(slot.t[:, kc * 512 + i * 128:kc * 512 + i * 128 + 128], nall[kc].ap) for kc in range(16)],
                         reads=[slot.v(0, SLOT)] + nall,
                         fine=([[slot.v(0, SLOT), nall[kc]] for kc in range(16)] if (mt == 0 and i == 0) else None))
                src = bk.v(0, 512)
                if mt == 0:
                    d = abuf(c, 30, 30 + T)
                    P.op("scalar", lambda e, bk=bk, d=d: e.copy(out=d.ap, in_=bk.t[:, 0:T]), reads=[src], writes=[d])
                elif mt == 1:
                    t = tp(4 + (i % 3), T)
                    d = abuf(c, 30, 30 + T)
                    P.op("scalar", lambda e, bk=bk, t=t: e.activation(out=t.ap, in_=bk.t[:, 0:T], func=AF.Sigmoid), reads=[src], writes=[t])
                    P.op("vector", lambda e, t=t, d=d: e.tensor_tensor(out=d.ap, in0=d.ap, in1=t.ap, op=ALU.mult), reads=[t, d], writes=[d])
                elif mt == 2:
                    d = pbuf(c, 16, 16 + T)
                    P.op("scalar", lambda e, bk=bk, d=d: e.copy(out=d.ap, in_=bk.t[:, 0:T]), reads=[src], writes=[d])
                elif mt == 3:
                    d = ub(c)
                    P.op("scalar", lambda e, bk=bk, d=d: e.activation(out=d.ap, in_=bk.t[:, 0:T], func=AF.Gelu), reads=[src], writes=[d])
                elif mt == 4:
                    d = vb(c)
                    P.op("scalar", lambda e, bk=bk, d=d: e.activation(out=d.ap, in_=bk.t[:, 0:T], func=AF.Gelu), reads=[src], writes=[d])
                elif mt == 5:
                    d = gb(c)
                    P.op("scalar", lambda e, bk=bk, d=d: e.copy(out=d.ap, in_=bk.t[:, 0:T]), reads=[src], writes=[d])
                elif mt == 6:
                    d = chb(c, 2, 2 + T)
                    P.op("scalar", lambda e, bk=bk, d=d: e.copy(out=d.ap, in_=bk.t[:, 0:T]), reads=[src], writes=[d])
                else:
                    d = chb(c, 2, 2 + T)
                    P.op("vector", lambda e, bk=bk, d=d: e.tensor_tensor(out=d.ap, in0=d.ap, in1=bk.t[:, 0:T], op=ALU.mult), reads=[src, d], writes=[d])

        mk = cc("mask", 0)

        def mask_and_save(bufs_fn, so, sn):
            if blk == 0:
                for c in range(4):
                    d = bufs_fn(c, HALO)
                    P.op("vector", lambda e, d=d: e.tensor_scalar(out=d.ap, in0=d.ap, scalar1=mk.ap, scalar2=None, op0=ALU.mult),
                         reads=[d, mk], writes=[d])
            for c in range(4):
                src_ = bufs_fn(c, None)
                P.op("vector", lambda e, c=c, src_=src_: e.tensor_copy(out=cstv(c, so, sn).ap, in_=src_.ap), reads=[src_], writes=[cstv(c, so, sn)])

        for mt in (0, 1):
            proj_tile(mt)
        mask_and_save(lambda c, h: abuf(c, 30, 30 + h) if h else abuf(c, T, T + 30), 0, 30)

        def cw(c, k):
            return cc("conv_w", (l * 4 + c) * 31 + k)

        for c in range(4):
            P.op("vector", lambda e, c=c: e.tensor_scalar(out=acc(c).ap, in0=abuf(c, 0, T).ap, scalar1=cw(c, 0).ap,
                                                          scalar2=cc("conv_b", l * 4 + c).ap, op0=ALU.mult, op1=ALU.add),
                 reads=[abuf(c, 0, T), cw(c, 0), cc("conv_b", l * 4 + c)], writes=[acc(c)])
        for k in range(1, 31):
            for c in range(4):
                P.op("vector", lambda e, c=c, k=k: e.scalar_tensor_tensor(out=acc(c).ap, in0=abuf(c, k, k + T).ap, scalar=cw(c, k).ap,
                                                                          in1=acc(c).ap, op0=ALU.mult, op1=ALU.add),
                     reads=[abuf(c, k, k + T), cw(c, k), acc(c)], writes=[acc(c)])
        for mt in (6, 5, 2, 3, 4, 7):
            proj_tile(mt)
        mask_and_save(lambda c, h: pbuf(c, 16, 16 + h) if h else pbuf(c, T, T + 16), 30, 16)
        mask_and_save(lambda c, h: chb(c, 2, 2 + h) if h else chb(c, T, T + 2), 46, 2)

        def ln_stats(src, c0, mean, rstd):
            sb = [hc(c0 + c, T) for c in range(4)]
            sqb = [hc(c0 + 4 + c, T) for c in range(4)]
            for c in range(4):
                P.op("scalar", lambda e, c=c: e.copy(out=sb[c].ap, in_=src(c).ap), reads=[src(c)], writes=[sb[c]])
                P.op("scalar", lambda e, c=c: e.activation(out=sqb[c].ap, in_=src(c).ap, func=AF.Square), reads=[src(c)], writes=[sqb[c]])
            bm, bq = stats_sum(sb, T), stats_sum(sqb, T)
            msq = tp(1, T)
            P.op("vector", lambda e: e.tensor_scalar(out=mean.ap, in0=bm.t[:, 0:T], scalar1=1.0 / BW, scalar2=None, op0=ALU.mult),
                 reads=[bm.v(0, 512)], writes=[mean])
            P.op("vector", lambda e: e.tensor_tensor(out=msq.ap, in0=mean.ap, in1=mean.ap, op=ALU.mult), reads=[mean], writes=[msq])
            rstd_from(bq, T, 1.0 / BW, LN_EPS, rstd, pre_sub=msq)

        def ln_apply(src, mean, rstd, gname, bname, out_fn, act):
            for c in range(4):
                t = tp(4 + (c % 3), T)
                P.op("vector", lambda e, c=c, t=t: e.tensor_tensor(out=t.ap, in0=src(c).ap, in1=mean.ap, op=ALU.subtract),
                     reads=[src(c), mean], writes=[t])
                P.op("vector", lambda e, t=t: e.tensor_tensor(out=t.ap, in0=t.ap, in1=rstd.ap, op=ALU.mult), reads=[t, rstd], writes=[t])
                o = out_fn(c)
                P.op("scalar", lambda e, c=c, t=t, o=o: e.activation(out=o.ap, in_=t.ap, func=act, bias=cc(bname, l * 4 + c).ap,
                                                                     scale=cc(gname, l * 4 + c).ap),
                     reads=[t, cc(bname, l * 4 + c), cc(gname, l * 4 + c)], writes=[o])

        sgu_mean, sgu_rstd = tp(7, T), tp(3, T)
        conv_mean, conv_rstd = tp(0, T), tp(2, T)
        ln_stats(vb, 24, sgu_mean, sgu_rstd)
        ln_stats(acc, 16, conv_mean, conv_rstd)

        def pool_branch():
            pt = [MS.v(PT0, PT0 + PW), MS.v(PT1, PT1 + PW)]
            for g, w in enumerate((2, 4, 8, 16)):
                cur = MS.v(PBUF + g * PW, PBUF + (g + 1) * PW)
                curo = PBUF + g * PW
                sh = 1
                k = 0
                while sh < w:
                    dst = pt[k % 2]
                    dsto = PT0 if k % 2 == 0 else PT1
                    lo = 2 * sh - 1
                    P.op("vector", lambda e, curo=curo, dsto=dsto, sh=sh, lo=lo: e.tensor_tensor(
                        out=MS.t[:, dsto + lo:dsto + 16 + T], in0=MS.t[:, curo + lo:curo + 16 + T],
                        in1=MS.t[:, curo + lo - sh:curo + 16 + T - sh], op=ALU.add),
                        reads=[cur], writes=[dst])
                    cur, curo = dst, dsto
                    sh *= 2
                    k += 1
                pin = pbuf(g, 16, 16 + T)
                pooled = hc(32 + g, T)
                if blk == 0:
                    q = tp(1, T)
                    P.op("vector", lambda e, curo=curo, q=q, w=w: e.tensor_scalar(out=q.ap, in0=MS.t[:, curo + 16:curo + 16 + T], scalar1=1.0 / w,
                                                                                 scalar2=None, op0=ALU.mult), reads=[cur], writes=[q])
                    co = CPP_OFF["corr"] + g * 16
                    P.op("vector", lambda e, q=q, co=co: e.tensor_tensor(out=q.ap[:, HALO:HALO + 16], in0=q.ap[:, HALO:HALO + 16],
                                                                        in1=cpp.t[:, co:co + 16], op=ALU.mult),
                         reads=[q, cpp.v(co, co + 16)], writes=[q])
                    P.op("vector", lambda e, q=q, pin=pin, pooled=pooled: e.tensor_tensor(out=pooled.ap, in0=q.ap, in1=pin.ap, op=ALU.subtract),
                         reads=[q, pin], writes=[pooled])
                else:
                    P.op("vector", lambda e, curo=curo, pin=pin, pooled=pooled, w=w: e.scalar_tensor_tensor(
                        out=pooled.ap, in0=MS.t[:, curo + 16:curo + 16 + T], scalar=1.0 / w, in1=pin.ap, op0=ALU.mult, op1=ALU.subtract),
                        reads=[cur, pin], writes=[pooled])
                bk = bank()
                mm_group(bk, T, [(poolw.t[:, l * 512 + g * 128:l * 512 + (g + 1) * 128], pooled.ap)],
                         reads=[poolw.v(l * 512, (l + 1) * 512), pooled])
                o = hc(4 + g, T)
                P.op("scalar", lambda e, bk=bk, o=o, g=g: e.activation(out=o.ap, in_=bk.t[:, 0:T], func=AF.Identity,
                                                                       scale=cc("pool_scale", l * 4 + g).ap),
                     reads=[bk.v(0, 512), cc("pool_scale", l * 4 + g)], writes=[o])


        def sgu_branch():
            ln_apply(vb, sgu_mean, sgu_rstd, "sgu_ln_g", "sgu_ln_b", lambda c: hc(36 + c, T), AF.Identity)
            vtok0 = 40 * TM
            npair = NT * 4
            done = 0
            while done < npair:
                cnt = min(8, npair - done)
                srcs = []
                for idx in range(done, done + cnt):
                    n, g = divmod(idx, 4)
                    srcs.append(hT.v((36 + g) * TM + n * 128, (36 + g) * TM + (n + 1) * 128))

                def fn(e, srcs=srcs):
                    ins = None
                    for i, s_ in enumerate(srcs):
                        ins = e.transpose(out=psb.t[:, i * 128:(i + 1) * 128], in_=s_.ap, identity=identb.t[:, 0:128])
                    return ins
                P.op("tensor", fn, reads=srcs + [identb.v(0, 128)], writes=[psb.v(0, 1024)])
                dst = hT.v(vtok0 + done * 128, vtok0 + (done + cnt) * 128)
                P.op("scalar", lambda e, dst=dst, cnt=cnt: e.copy(out=dst.ap, in_=psb.t[:, 0:cnt * 128]), reads=[psb.v(0, 1024)], writes=[dst])
                done += cnt
            for g in range(4):
                bk = bank()

                def fn(e, bk=bk, g=g):
                    ins = None
                    for n in range(NT):
                        o = vtok0 + (n * 4 + g) * 128
                        ins = e.matmul(bk.t[:, n * 128:(n + 1) * 128], lhsT=hT.t[:, o:o + 128],
                                       rhs=sguw.t[:, l * 512 + g * 128:l * 512 + (g + 1) * 128], start=True, stop=True)
                    return ins
                P.op("tensor", fn, reads=[hT.v(vtok0, vtok0 + npair * 128), sguw.v(l * 512, (l + 1) * 512)], writes=[bk.v(0, 512)])
                t = tp(4 + (g % 3), T)
                bo = l * 512 + g * 128
                P.op("vector", lambda e, bk=bk, t=t, bo=bo: e.tensor_tensor(
                    out=t.ap.rearrange("p (n t) -> p n t", n=NT), in0=bk.t[:, 0:T].rearrange("p (n t) -> p n t", n=NT),
                    in1=bsb.t[:, bo:bo + 128].unsqueeze(1).broadcast_to([128, NT, 128]), op=ALU.add),
                    reads=[bk.v(0, 512), bsb.v(bo, bo + 128)], writes=[t])
                o = hc(8 + g, T)
                P.op("vector", lambda e, t=t, o=o, g=g: e.tensor_tensor(out=o.ap, in0=t.ap, in1=ub(g).ap, op=ALU.mult),
                     reads=[t, ub(g)], writes=[o])


        def conv_ln_branch():
            ln_apply(acc, conv_mean, conv_rstd, "conv_ln_g", "conv_ln_b", lambda c: hc(c, T), AF.Silu)


        def sconv_branch():
            def sw(c, k):
                return cc("sconv_w", (l * 4 + c) * 3 + k)

            for c in range(4):
                a2 = acc(c)
                P.op("vector", lambda e, c=c, a2=a2: e.tensor_scalar(out=a2.ap, in0=chb(c, 0, T).ap, scalar1=sw(c, 0).ap, scalar2=None, op0=ALU.mult),
                     reads=[chb(c, 0, T), sw(c, 0)], writes=[a2])
            for k in (1, 2):
                for c in range(4):
                    a2 = acc(c)
                    P.op("vector", lambda e, c=c, k=k, a2=a2: e.scalar_tensor_tensor(out=a2.ap, in0=chb(c, k, k + T).ap, scalar=sw(c, k).ap,
                                                                                    in1=a2.ap, op0=ALU.mult, op1=ALU.add),
                         reads=[chb(c, k, k + T), sw(c, k), a2], writes=[a2])
            for c in range(4):
                o = hc(12 + c, T)
                P.op("vector", lambda e, c=c, o=o: e.tensor_tensor(out=o.ap, in0=acc(c).ap, in1=gb(c).ap, op=ALU.mult),
                     reads=[acc(c), gb(c)], writes=[o])


        yb = [hc(c, T) for c in range(16)]
        BORDER = (1, 2, 0, 3)

        def macc(i):
            return MS.v(ACC2 + i * TM, ACC2 + i * TM + T)

        def gate_step(qd, b):
            first, last = (b == BORDER[0]), (b == BORDER[-1])
            brs = getbr(l, tb + 8 + qd * 5, qd, b)
            gs = getw(tb + 8 + qd * 5 + 1 + b, lambda st, qd=qd, b=b: [
                (slot3(st, 16, 512), kc_view(gatew_d[l, b], qd * 512, 512))])
            for i in range(4):
                dc = 4 * qd + i
                bg, by = bank(), bank()
                mm_group(bg, T, [(gs.t[:, kc * 512 + i * 128:kc * 512 + i * 128 + 128], nall[kc].ap) for kc in range(16)],
                         reads=[gs.v(0, SLOT)] + nall)
                mm_group(by, T, [(brs.t[:, c * 512 + i * 128:c * 512 + i * 128 + 128], yb[b * 4 + c].ap) for c in range(4)],
                         reads=[brs.v(0, 2048)] + yb[b * 4:b * 4 + 4])
                sg = tp(4 + (i % 2), T)
                gbv = cc("gate_b", (l * 4 + b) * 16 + dc)
                P.op("scalar", lambda e, bg=bg, sg=sg, gbv=gbv: e.activation(out=sg.ap, in_=bg.t[:, 0:T], func=AF.Sigmoid, bias=gbv.ap, scale=1.0),
                     reads=[bg.v(0, 512), gbv], writes=[sg])
                if first:
                    P.op("vector", lambda e, by=by, sg=sg, i=i: e.tensor_tensor(out=macc(i).ap, in0=sg.ap, in1=by.t[:, 0:T], op=ALU.mult),
                         reads=[by.v(0, 512), sg], writes=[macc(i)])
                else:
                    t = tp(6 if (i % 2) else 1, T)
                    P.op("vector", lambda e, by=by, sg=sg, t=t: e.tensor_tensor(out=t.ap, in0=sg.ap, in1=by.t[:, 0:T], op=ALU.mult),
                         reads=[by.v(0, 512), sg], writes=[t])
                    o = hc(16 + dc, T) if last else macc(i)
                    P.op("vector", lambda e, t=t, o=o, i=i: e.tensor_tensor(out=o.ap, in0=macc(i).ap, in1=t.ap, op=ALU.add),
                         reads=[t, macc(i)], writes=[o])

        if debug and l == 0 and blk == 0:
            pool_branch(); sgu_branch(); conv_ln_branch(); sconv_branch()
            P.dma("sync", [(dbg_d, hT.t[:, 0:16 * TM])], dsem["ld"], reads=[hT.v(0, 16 * TM)])
            for qd in range(4):
                for b in BORDER:
                    gate_step(qd, b)
        else:
            pool_branch()
            gate_step(0, 1)
            sgu_branch()
            gate_step(0, 2)
            conv_ln_branch()
            gate_step(0, 0)
            sconv_branch()
            gate_step(0, 3)
            for qd in range(1, 4):
                for b in BORDER:
                    gate_step(qd, b)

        mg = [hc(16 + c, T) for c in range(16)]
        if debug and l == 0 and blk == 0:
            P.dma("sync", [(dbg2_d, hT.t[:, 16 * TM:32 * TM])], dsem["ld"], reads=[hT.v(16 * TM, 32 * TM)])
        for qd in range(4):
            ws_ = getw(tb + 28 + qd, lambda st, qd=qd: [(slot3(st, 16, 512), kc_view(wo_d[l], qd * 512, 512))])
            for i in range(4):
                dc = 4 * qd + i
                bk = bank()
                mm_group(bk, T, [(ws_.t[:, kc * 512 + i * 128:kc * 512 + i * 128 + 128], mg[kc].ap) for kc in range(16)],
                         reads=[ws_.v(0, SLOT)] + mg)
                evac_y(bk, dc, T)
        postnorm(l, 1, T)

    prologue_ada()
    t0 = 0
    for blk, T in enumerate(blocks):
        NT = T // 128
        wstate["blk"] = blk
        if blk > 0 and blk % 2 == 0:
            P.new_epoch()
        stg = MS.v(0, NT * 2048)
        P.dma("sync", [(MS.t[:, n * 2048:(n + 1) * 2048], x_d[t0 + n * 128:t0 + (n + 1) * 128, :]) for n in range(NT)],
              dsem["xin"], writes=[stg])
        for n in range(NT):
            for q in range(4):
                bk = bank()

                def fn(e, bk=bk, n=n, q=q):
                    ins = None
                    for j in range(4):
                        dc = 4 * q + j
                        ins = e.transpose(out=bk.t[:, j * 128:(j + 1) * 128], in_=MS.t[:, n * 2048 + dc * 128:n * 2048 + (dc + 1) * 128],
                                          identity=ident.t[:, 0:128])
                    return ins
                P.op("tensor", fn, reads=[MS.v(n * 2048, (n + 1) * 2048), ident.v(0, 128)], writes=[bk.v(0, 512)])
                d = xT.v3(4 * q * TM, 4, TM, sub=(n * 128, (n + 1) * 128))
                P.op("vector" if (q % 2) else "scalar",
                     (lambda e, bk=bk, d=d: e.tensor_copy(out=d.ap, in_=bk.t[:, 0:512].rearrange("p (a b) -> p a b", a=4))) if (q % 2) else
                     (lambda e, bk=bk, d=d: e.copy(out=d.ap, in_=bk.t[:, 0:512].rearrange("p (a b) -> p a b", a=4))),
                     reads=[bk.v(0, 512)], writes=[d])
        for l in range(n_layers):
            if "ffn0" in stages:
                ffn(l, 0, 0, T)
            if "mix" in stages:
                mixer(l, T, blk)
            if "ffn1" in stages:
                ffn(l, 2, 1, T)
        n_first = 1 if blk == 0 else 0
        stg = MS.v(0, NT * 2048)
        for n in range(n_first, NT):
            for q in range(4):
                bk = bank()

                def fn(e, bk=bk, n=n, q=q):
                    ins = None
                    for j in range(4):
                        dc = 4 * q + j
                        ins = e.transpose(out=bk.t[:, j * 128:(j + 1) * 128], in_=xT.t[:, dc * TM + n * 128:dc * TM + (n + 1) * 128],
                                          identity=ident.t[:, 0:128])
                    return ins
                P.op("tensor", fn, reads=[xT.v(4 * q * TM, (4 * q + 4) * TM), ident.v(0, 128)], writes=[bk.v(0, 512)])
                d = MS.v(n * 2048 + q * 512, n * 2048 + (q + 1) * 512)
                if q % 2:
                    P.op("vector", lambda e, bk=bk, d=d: e.tensor_copy(out=d.ap, in_=bk.t[:, 0:512]), reads=[bk.v(0, 512)], writes=[d])
                else:
                    P.op("scalar", lambda e, bk=bk, d=d: e.copy(out=d.ap, in_=bk.t[:, 0:512]), reads=[bk.v(0, 512)], writes=[d])
        if NT > n_first:
            tok = P.dma("sync", [(out_d[t0 - HALO + n * 128:t0 - HALO + (n + 1) * 128, :], MS.t[:, n * 2048:(n + 1) * 2048])
                                 for n in range(n_first, NT)], dsem["xout"], reads=[MS.v(n_first * 2048, NT * 2048)])
            last_tok = tok
        t0 += T
    P.wait_on("sync", last_tok)
    P.emit()
    return nc


BLOCKS_FULL = [384, 384, 384, 384, 384, 256]


def _host_inputs(inp, core, blocks):
    ntok = sum(blocks)
    x = np.asarray(inp["x"], np.float32)[0]
    g0 = core * TOK_PER_CORE - HALO
    xs = np.zeros((ntok, D), np.float32)
    lo = max(g0, 0)
    xs[lo - g0:, :] = x[lo:g0 + ntok]
    f = lambda k: np.ascontiguousarray(np.asarray(inp[k], np.float32))
    m = {
        "x": xs,
        "cpp": _build_cpp(inp, core),
        "bsb": np.ascontiguousarray(np.broadcast_to(np.asarray(inp["sgu_b_s"], np.float32).reshape(1, NL * 512), (128, NL * 512))),
        "ident": np.eye(128, dtype=np.float32),
        "cmask": np.triu(np.ones((128, 128), np.float32)),
        "ada_w": f("ada_w"), "ffn_w_in": f("ffn_w_in"), "ffn_w_out": f("ffn_w_out"), "mix_w_in": f("mix_w_in"),
        "gate_w": f("gate_w"), "branch_w_out": f("branch_w_out"), "w_o": f("w_o"), "pool_group_w": f("pool_group_w"),
        "sgu_w_sT": np.ascontiguousarray(np.asarray(inp["sgu_w_s"], np.float32).transpose(0, 1, 3, 2)),
    }
    return m


def kernel(**inp):
    blocks = BLOCKS_FULL
    nc = build(blocks)
    in_maps = [_host_inputs(inp, c, blocks) for c in range(NCORES)]
    res = run_bass_kernel_spmd(nc, in_maps, core_ids=list(range(NCORES)))
    out = np.concatenate([np.asarray(r["out"], np.float32) for r in res.results], axis=0)
    return out.reshape(1, NCORES * TOK_PER_CORE, D)
```

```python
import numpy as np
import concourse.bass as bass
import concourse.mybir as mybir
from concourse.bass_utils import run_bass_kernel_spmd

F32 = mybir.dt.float32
BF16 = mybir.dt.bfloat16
AF = mybir.ActivationFunctionType
ALU = mybir.AluOpType
AX = mybir.AxisListType

ENGS = ("tensor", "scalar", "vector", "gpsimd", "sync")

D = 2048
FF = 5632
BW = 512
NL = 2
NCORES = 8
TOK_PER_CORE = 2048
HALO = 128
TMAX = 384
RMS_EPS = 1e-6
LN_EPS = 1e-5
SLOT = 8192
NSLOT = 3


class V:
    __slots__ = ("ap", "key")

    def __init__(self, ap, key):
        self.ap = ap
        self.key = key


class Plan:
    def __init__(self, nc):
        self.nc = nc
        self.ops = {e: [] for e in ENGS}
        self.semname = {}
        self.cnt = {}
        self.seen = {e: {} for e in ENGS}
        self.hist = {}
        self.recs = {}
        self.handles = {}
        self.nsem = 0
        for e in ENGS:
            self._new_eng_sem(e)

    def new_sem(self, name):
        h = self.nc.alloc_semaphore(name)
        self.handles[name] = h
        self.cnt[name] = 0
        self.nsem += 1
        return name

    def _new_eng_sem(self, e):
        name = f"p_{e}_{self.nsem}"
        self.new_sem(name)
        self.semname[e] = name

    def new_epoch(self):
        for e in ENGS:
            self._new_eng_sem(e)

    def _ov(self, key):
        name, lo, hi = key
        lst = self.recs.setdefault(name, [])
        return lst, [r for r in lst if r[0] < hi and lo < r[1]]

    def _deps_for(self, reads, writes):
        deps = {}

        def add(s, v):
            if deps.get(s, 0) < v:
                deps[s] = v

        for k in reads:
            for r in self._ov(k)[1]:
                if r[2] is not None:
                    add(*r[2])
        for k in writes:
            for r in self._ov(k)[1]:
                if r[2] is not None:
                    add(*r[2])
                for s, v in r[3].items():
                    add(s, v)
        return deps

    def _record(self, reads, writes, tok):
        s, v = tok
        for k in reads:
            lst, ov = self._ov(k)
            exact = None
            for r in ov:
                if r[0] == k[1] and r[1] == k[2]:
                    exact = r
            if exact is None:
                exact = [k[1], k[2], None, {}]
                lst.append(exact)
            if exact[3].get(s, 0) < v:
                exact[3][s] = v
        for k in writes:
            lst, ov = self._ov(k)
            for r in ov:
                if k[1] <= r[0] and r[1] <= k[2]:
                    lst.remove(r)
            lst.append([k[1], k[2], (s, v), {}])

    def _waits(self, eng, deps, skip_self=False):
        waits = []
        seen = self.seen[eng]
        own = self.semname[eng]
        for s, v in deps.items():
            if skip_self and s == own:
                continue
            if seen.get(s, 0) >= v:
                if s != own:
                    continue
                if seen.get(("own", s), 0) >= v:
                    continue
            waits.append((s, v))
            if s == own:
                seen[("own", s)] = v
            if seen.get(s, 0) < v:
                seen[s] = v
            h = self.hist.get((s, v))
            if h:
                for s2, v2 in h.items():
                    if seen.get(s2, 0) < v2:
                        seen[s2] = v2
        return waits

    @staticmethod
    def _keys(lst):
        return [x.key if isinstance(x, V) else x for x in lst]

    def op(self, eng, fn, reads=(), writes=()):
        rk = self._keys(reads)
        wk = self._keys(writes)
        wk = wk + [k for k in rk if k[0].startswith("ps")]
        rk = [k for k in rk if not k[0].startswith("ps")]
        deps = self._deps_for(rk, wk)
        waits = self._waits(eng, deps, skip_self=(eng == "tensor"))
        s = self.semname[eng]
        self.cnt[s] += 1
        v = self.cnt[s]
        self.seen[eng][s] = v
        self.hist[(s, v)] = {k: x for k, x in self.seen[eng].items() if not isinstance(k, tuple)}
        self.ops[eng].append((waits, fn, (s, 1)))
        self._record(rk, wk, (s, v))
        return (s, v)

    def dma(self, eng, pairs, dsem, reads=(), writes=()):
        rk = self._keys(reads)
        wk = self._keys(writes)
        deps = self._deps_for(rk, wk)
        if self.cnt[dsem] > 0 and deps.get(dsem, 0) < self.cnt[dsem]:
            deps[dsem] = self.cnt[dsem]
        waits = self._waits(eng, deps)
        self.cnt[dsem] += 16 * len(pairs)
        v = self.cnt[dsem]
        self.hist[(dsem, v)] = {k: x for k, x in self.seen[eng].items() if not isinstance(k, tuple)}
        h = self.handles[dsem]

        def fn(e, pairs=pairs, h=h):
            for (o, i) in pairs:
                e.dma_start(out=o, in_=i).then_inc(h, 16)
            return None

        self.ops[eng].append((waits, fn, None))
        self._record(rk, wk, (dsem, v))
        return (dsem, v)

    def wait_on(self, eng, tok):
        waits = self._waits(eng, {tok[0]: tok[1]})
        if waits:
            self.ops[eng].append((waits, None, None))

    def emit(self):
        nc = self.nc
        handles = self.handles
        ops = self.ops
        with nc.Block() as block:
            def run(e, lst):
                for waits, fn, inc in lst:
                    for s, v in waits:
                        e.wait_ge(handles[s], v)
                    if fn is None:
                        continue
                    ins = fn(e)
                    if inc is not None:
                        ins.then_inc(handles[inc[0]], inc[1])

            @block.tensor
            def _(e):
                run(e, ops["tensor"])

            @block.scalar
            def _(e):
                run(e, ops["scalar"])

            @block.vector
            def _(e):
                run(e, ops["vector"])

            @block.gpsimd
            def _(e):
                run(e, ops["gpsimd"])

            @block.sync
            def _(e):
                run(e, ops["sync"])


class Tn:
    def __init__(self, nc, name, cols, dtype, psum=False, parts=128):
        self.name = name
        self.cols = cols
        if psum:
            self.t = nc.alloc_psum_tensor(name, [parts, cols], dtype)
        else:
            self.t = nc.alloc_sbuf_tensor("sb_" + name, [parts, cols], dtype)

    def v(self, lo, hi):
        return V(self.t[:, lo:hi], (self.name, lo, hi))

    def v3(self, lo, a, b, sub=None):
        ap = self.t[:, lo:lo + a * b].rearrange("p (a b) -> p a b", a=a)
        if sub is not None:
            ap = ap[:, :, sub[0]:sub[1]]
            return V(ap, (self.name, lo + sub[0], lo + (a - 1) * b + sub[1]))
        return V(ap, (self.name, lo, lo + a * b))


def _cpp_layout():
    off = {}
    n = 0

    def add(name, cols):
        nonlocal n
        off[name] = n
        n += cols

    add("c", 16)
    add("ada_b", NL * 144)
    add("pre_g", NL * 3 * 16)
    add("post_g", NL * 3 * 16)
    add("gate_b", NL * 4 * 16)
    add("conv_w", NL * 4 * 31)
    add("conv_b", NL * 4)
    add("conv_ln_g", NL * 4)
    add("conv_ln_b", NL * 4)
    add("pool_scale", NL * 4)
    add("sgu_ln_g", NL * 4)
    add("sgu_ln_b", NL * 4)
    add("sconv_w", NL * 4 * 3)
    add("mask", 1)
    add("corr", 4 * 16)
    return off, n


CPP_OFF, CPP_N = _cpp_layout()


def _pp(a):
    a = np.asarray(a, dtype=np.float32)
    lead = a.shape[:-1]
    n = a.shape[-1] // 128
    a = a.reshape(lead + (n, 128))
    a = np.moveaxis(a, -1, 0)
    return np.ascontiguousarray(a.reshape(128, -1))


def _build_cpp(inp, core):
    cpp = np.zeros((128, CPP_N), np.float32)

    def put(name, arr):
        cpp[:, CPP_OFF[name]:CPP_OFF[name] + arr.shape[1]] = arr

    put("c", _pp(inp["c"][0]))
    put("ada_b", _pp(inp["ada_b"]))
    put("pre_g", _pp(inp["pre_g"]))
    put("post_g", _pp(inp["post_g"]))
    put("gate_b", _pp(inp["gate_b"]))
    cw = np.asarray(inp["conv_w"], np.float32).reshape(NL, 31, 4, 128)
    put("conv_w", np.ascontiguousarray(cw.transpose(3, 0, 2, 1).reshape(128, -1)))
    for nm in ("conv_b", "conv_ln_g", "conv_ln_b", "pool_scale", "sgu_ln_g", "sgu_ln_b"):
        put(nm, _pp(inp[nm]))
    sw = np.asarray(inp["sconv_w"], np.float32).reshape(NL, 3, 4, 128)
    put("sconv_w", np.ascontiguousarray(sw.transpose(3, 0, 2, 1).reshape(128, -1)))
    cpp[:, CPP_OFF["mask"]] = 0.0 if core == 0 else 1.0
    corr = np.ones((4, 16), np.float32)
    if core == 0:
        for g, w in enumerate((2, 4, 8, 16)):
            for t in range(16):
                corr[g, t] = float(w) / float(min(t + 1, w))
    cpp[:, CPP_OFF["corr"]:CPP_OFF["corr"] + 64] = corr.reshape(1, 64)
    return cpp


def build(blocks, n_layers=NL, stages=("ffn0", "mix", "ffn1"), scratch=True, debug=False):
    ntok = sum(blocks)
    nc = bass.Bass("TRN2", target_bir_lowering=False)
    P = Plan(nc)

    def din(name, shape, dt=F32):
        return nc.dram_tensor(name, list(shape), dt, kind="ExternalInput").ap()

    x_d = din("x", [ntok, D])
    out_d = nc.dram_tensor("out", [ntok - HALO, D], F32, kind="ExternalOutput").ap()
    cpp_d = din("cpp", [128, CPP_N])
    bsb_d = din("bsb", [128, NL * 512])
    ident_d = din("ident", [128, 128])
    cmask_d = din("cmask", [128, 128])
    ada_w_d = din("ada_w", [NL, D, 18432])
    win_d = din("ffn_w_in", [NL, 2, D, 2 * FF])
    wout_d = din("ffn_w_out", [NL, 2, FF, D])
    mixw_d = din("mix_w_in", [NL, D, 4096])
    gatew_d = din("gate_w", [NL, 4, D, D])
    brw_d = din("branch_w_out", [NL, 4, BW, D])
    wo_d = din("w_o", [NL, D, D])
    poolw_d = din("pool_group_w", [NL, 4, 128, 128])
    sguw_d = din("sgu_w_sT", [NL, 4, 128, 128])

    if debug:
        dbg_d = nc.dram_tensor("dbg", [128, 16 * TMAX], BF16, kind="ExternalOutput").ap()
        dbg2_d = nc.dram_tensor("dbg2", [128, 16 * TMAX], BF16, kind="ExternalOutput").ap()
    TILES_PER_LAYER = 100
    if scratch:
        wsc_l = [nc.dram_tensor(f"wscratch{l}", [TILES_PER_LAYER, 128, SLOT], BF16, kind="Internal").ap() for l in range(NL)]
        wsc_d = {l * TILES_PER_LAYER + i: wsc_l[l][i] for l in range(NL) for i in range(TILES_PER_LAYER)}

    TM = TMAX
    xT = Tn(nc, "xT", 16 * TM, F32)
    MS = Tn(nc, "MS", 12288, F32)
    nT = Tn(nc, "nT", 16 * TM, BF16)
    hT = Tn(nc, "hT", 44 * TM, BF16)
    wring = [Tn(nc, f"w{i}", SLOT, BF16) for i in range(NSLOT)]
    wsem = [P.new_sem(f"ws{i}") for i in range(NSLOT)]
    wbsem = [P.new_sem(f"wb{i}") for i in range(NSLOT)]
    cpp = Tn(nc, "cpp", CPP_N, F32)
    bsb = Tn(nc, "bsb", NL * 512, F32)
    ident = Tn(nc, "ident", 128, F32)
    identb = Tn(nc, "identb", 128, BF16)
    ones = Tn(nc, "ones", 128, BF16)
    cmask = Tn(nc, "cmask", 128, F32)
    poolw = Tn(nc, "poolw", NL * 512, BF16)
    sguw = Tn(nc, "sguw", NL * 512, BF16)
    adar = Tn(nc, "adar", NL * 144, F32)
    cond = Tn(nc, "cond", 16, F32)
    prm = Tn(nc, "prm", 3 * NL * 48, F32)
    TP = Tn(nc, "TP", 8 * TM, F32)
    cst = Tn(nc, "cst", NL * 4 * 48, F32)
    NPS = 7
    ps = [Tn(nc, f"ps{i}", 512, F32, psum=True) for i in range(NPS)]
    psb = Tn(nc, "psb", 1024, BF16, psum=True)
    bank_i = [0]

    def bank():
        b = ps[bank_i[0] % NPS]
        bank_i[0] += 1
        return b

    dsem = {k: P.new_sem(k) for k in ("ld", "xin", "xout")}

    def cc(name, idx, n=1):
        o = CPP_OFF[name] + idx
        return cpp.v(o, o + n)

    def xc(c, T):
        return xT.v(c * TM, c * TM + T)

    def nc_(c, T):
        return nT.v(c * TM, c * TM + T)

    def hc(c, T):
        return hT.v(c * TM, c * TM + T)

    def yc(c, T):
        return MS.v(c * TM, c * TM + T)

    def tp(i, T):
        return TP.v(i * TM, i * TM + T)

    P.dma("sync", [(cpp.t[:], cpp_d), (bsb.t[:], bsb_d), (ident.t[:], ident_d), (cmask.t[:], cmask_d)], dsem["ld"],
          writes=[cpp.v(0, CPP_N), bsb.v(0, NL * 512), ident.v(0, 128), cmask.v(0, 128)])
    P.op("vector", lambda e: e.tensor_copy(out=identb.t[:], in_=ident.t[:]), reads=[ident.v(0, 128)], writes=[identb.v(0, 128)])
    P.op("vector", lambda e: e.memset(ones.t[:], 1.0), writes=[ones.v(0, 128)])
    P.op("vector", lambda e: e.memset(cst.t[:], 0.0), writes=[cst.v(0, NL * 4 * 48)])
    for l in range(n_layers):
        stg = MS.v(0, 1024)
        P.dma("sync", [(MS.t[:, 0:512].rearrange("p (g d) -> p g d", g=4), poolw_d[l].rearrange("g c d -> c g d")),
                       (MS.t[:, 512:1024].rearrange("p (g t) -> p g t", g=4), sguw_d[l].rearrange("g j t -> j g t"))],
              dsem["ld"], writes=[stg])
        P.op("vector", lambda e, l=l: e.tensor_copy(out=poolw.t[:, l * 512:(l + 1) * 512], in_=MS.t[:, 0:512]),
             reads=[MS.v(0, 512)], writes=[poolw.v(l * 512, (l + 1) * 512)])
        P.op("vector", lambda e, l=l: e.tensor_tensor(
            out=sguw.t[:, l * 512:(l + 1) * 512].rearrange("p (g t) -> p g t", g=4),
            in0=MS.t[:, 512:1024].rearrange("p (g t) -> p g t", g=4),
            in1=cmask.t[:, 0:128].unsqueeze(1).broadcast_to([128, 4, 128]), op=ALU.mult),
            reads=[MS.v(512, 1024), cmask.v(0, 128)], writes=[sguw.v(l * 512, (l + 1) * 512)])

    wstate = {"i": 0, "converted": set(), "pinned": set(), "blk": 0, "br": 0}

    def getw(tile_id, pairs_fn, pin=False):
        si = wstate["i"] % NSLOT
        wstate["i"] += 1
        while si in wstate["pinned"]:
            si = wstate["i"] % NSLOT
            wstate["i"] += 1
        if pin:
            wstate["pinned"].add(si)
        slot = wring[si]
        full = slot.v(0, SLOT)
        if scratch and tile_id is not None and tile_id in wstate["converted"]:
            P.dma("sync", [(slot.t[:], wsc_d[tile_id])], wsem[si], reads=[("wsc", tile_id, tile_id + 1)], writes=[full])
        else:
            P.dma("gpsimd", pairs_fn(slot.t), wsem[si], writes=[full])
            if scratch and tile_id is not None:
                P.dma("sync", [(wsc_d[tile_id], slot.t[:])], wbsem[si], reads=[full], writes=[("wsc", tile_id, tile_id + 1)])
                wstate["converted"].add(tile_id)
        return slot

    brbuf = [Tn(nc, f"brb{i}", 2048, BF16) for i in range(2)]
    brsem = [P.new_sem(f"brs{i}") for i in range(2)]
    brwb = [P.new_sem(f"brw{i}") for i in range(2)]

    def getbr(l, tile_id, qd, b):
        k = wstate["br"] % 2
        wstate["br"] += 1
        buf = brbuf[k]
        full = buf.v(0, 2048)
        key = ("wscb", tile_id * 4 + b, tile_id * 4 + b + 1)
        img = wsc_d[tile_id][:, b * 2048:(b + 1) * 2048] if scratch else None
        if scratch and (tile_id, b) in wstate["converted"]:
            P.dma("sync", [(buf.t[:], img)], brsem[k], reads=[key], writes=[full])
        else:
            P.dma("gpsimd", [(buf.t[:, 0:2048].rearrange("p (c d) -> p c d", c=4),
                              brw_d[l, b][:, qd * 512:(qd + 1) * 512].rearrange("(c p) d -> p c d", p=128))], brsem[k], writes=[full])
            if scratch:
                P.dma("sync", [(img, buf.t[:])], brwb[k], reads=[full], writes=[key])
                wstate["converted"].add((tile_id, b))
        return buf

    def kc_view(dram2d, c0, ncols):
        return dram2d.rearrange("(kc p) f -> p kc f", p=128)[:, :, c0:c0 + ncols]

    def slot3(slot_t, a, b):
        return slot_t[:, 0:a * b].rearrange("p (a b) -> p a b", a=a)

    def prologue_ada():
        P.op("scalar", lambda e: e.activation(out=cond.t[:], in_=cpp.t[:, CPP_OFF["c"]:CPP_OFF["c"] + 16], func=AF.Silu),
             reads=[cc("c", 0, 16)], writes=[cond.v(0, 16)])
        crep = hT.v(0, 2048)
        P.op("vector", lambda e: e.tensor_copy(out=hT.t[:, 0:2048].rearrange("p (k m) -> p k m", k=16),
                                               in_=cond.t[:, 0:16].unsqueeze(2).broadcast_to([128, 16, 128])),
             reads=[cond.v(0, 16)], writes=[crep])
        for l in range(n_layers):
            for n in range(36):
                slot = getw(None, lambda st, l=l, n=n: [(slot3(st, 16, 512), kc_view(ada_w_d[l], n * 512, 512))])
                bk = bank()

                def fn(e, slot=slot, bk=bk):
                    ins = None
                    for kc in range(16):
                        ins = e.matmul(bk.t[:, 0:512], lhsT=hT.t[:, kc * 128:(kc + 1) * 128],
                                       rhs=slot.t[:, kc * 512:(kc + 1) * 512], start=(kc == 0), stop=(kc == 15))
                    return ins
                P.op("tensor", fn, reads=[slot.v(0, SLOT), crep], writes=[bk.v(0, 512)])
                tmp = MS.v(2048, 2560)
                P.op("vector", lambda e, bk=bk: e.tensor_tensor(
                    out=MS.t[:, 2048:2560].rearrange("p (j q) -> p j q", j=4),
                    in0=bk.t[:, 0:512].rearrange("p (j q) -> p j q", j=4),
                    in1=ident.t[:, 0:128].unsqueeze(1).broadcast_to([128, 4, 128]), op=ALU.mult),
                    reads=[bk.v(0, 512), ident.v(0, 128)], writes=[tmp])
                o = l * 144 + n * 4
                P.op("vector", lambda e, o=o: e.tensor_reduce(
                    out=adar.t[:, o:o + 4], in_=MS.t[:, 2048:2560].rearrange("p (j q) -> p j q", j=4),
                    axis=AX.X, op=ALU.add),
                    reads=[tmp], writes=[adar.v(o, o + 4)])
        nA = n_layers * 144
        P.op("vector", lambda e: e.tensor_tensor(out=adar.t[:, 0:nA], in0=adar.t[:, 0:nA],
                                                 in1=cpp.t[:, CPP_OFF["ada_b"]:CPP_OFF["ada_b"] + nA], op=ALU.add),
             reads=[adar.v(0, nA), cc("ada_b", 0, nA)], writes=[adar.v(0, nA)])
        for l in range(n_layers):
            for s in range(3):
                base = l * 144 + s * 48
                po = (l * 3 + s) * 16
                A0, S0, B0 = po, NL * 48 + po, 2 * NL * 48 + po
                P.op("vector", lambda e, base=base, po=po, A0=A0: e.scalar_tensor_tensor(
                    out=prm.t[:, A0:A0 + 16], in0=adar.t[:, base + 16:base + 32], scalar=1.0,
                    in1=cpp.t[:, CPP_OFF["pre_g"] + po:CPP_OFF["pre_g"] + po + 16], op0=ALU.add, op1=ALU.mult),
                    reads=[adar.v(base, base + 48), cc("pre_g", po, 16)], writes=[prm.v(A0, A0 + 16)])
                P.op("vector", lambda e, base=base, S0=S0: e.tensor_copy(out=prm.t[:, S0:S0 + 16], in_=adar.t[:, base:base + 16]),
                     reads=[adar.v(base, base + 48)], writes=[prm.v(S0, S0 + 16)])
                P.op("vector", lambda e, base=base, po=po, B0=B0, s=s: e.scalar_tensor_tensor(
                    out=prm.t[:, B0:B0 + 16], in0=adar.t[:, base + 32:base + 48], scalar=(1.0 if s == 1 else 0.5),
                    in1=cpp.t[:, CPP_OFF["post_g"] + po:CPP_OFF["post_g"] + po + 16], op0=ALU.mult, op1=ALU.mult),
                    reads=[adar.v(base, base + 48), cc("post_g", po, 16)], writes=[prm.v(B0, B0 + 16)])

    def prmv(kind, l, s, dc):
        o = kind * NL * 48 + (l * 3 + s) * 16 + dc
        return prm.v(o, o + 1)

    def mm_group(bk, T, pairs, reads, col0=0, fine=None):
        if fine is not None:
            n = len(pairs)
            tok = None
            for i, (l_, r_) in enumerate(pairs):
                def fn1(e, l_=l_, r_=r_, i=i, n=n):
                    return e.matmul(bk.t[:, col0:col0 + T], lhsT=l_, rhs=r_, start=(i == 0), stop=(i == n - 1))
                tok = P.op("tensor", fn1, reads=fine[i], writes=[bk.v(0, 512)])
            return tok

        def fn(e, pairs=pairs, bk=bk):
            ins = None
            n = len(pairs)
            for i, (l_, r_) in enumerate(pairs):
                ins = e.matmul(bk.t[:, col0:col0 + T], lhsT=l_, rhs=r_, start=(i == 0), stop=(i == n - 1))
            return ins
        return P.op("tensor", fn, reads=reads, writes=[bk.v(0, 512)])

    def stats_sum(src_chunks, T):
        bk = bank()
        mm_group(bk, T, [(ones.t[:, 0:128], s.ap) for s in src_chunks], reads=None,
                 fine=[[ones.v(0, 128), s] for s in src_chunks])
        return bk

    def rstd_from(bk, T, scale, eps, out_v, pre_sub=None):
        if pre_sub is None:
            P.op("scalar", lambda e: e.activation(out=out_v.ap, in_=bk.t[:, 0:T], func=AF.Ln, bias=float(eps), scale=float(scale)),
                 reads=[bk.v(0, 512)], writes=[out_v])
        else:
            P.op("vector", lambda e: e.scalar_tensor_tensor(out=out_v.ap, in0=bk.t[:, 0:T], scalar=float(scale), in1=pre_sub.ap,
                                                            op0=ALU.mult, op1=ALU.subtract),
                 reads=[bk.v(0, 512), pre_sub], writes=[out_v])
            P.op("scalar", lambda e: e.activation(out=out_v.ap, in_=out_v.ap, func=AF.Ln, bias=float(eps), scale=1.0),
                 reads=[out_v], writes=[out_v])
        P.op("scalar", lambda e: e.activation(out=out_v.ap, in_=out_v.ap, func=AF.Exp, scale=-0.5), reads=[out_v], writes=[out_v])

    def prenorm(l, s, T):
        sq = [hc(c, T) for c in range(16)]
        for c in range(16):
            P.op("scalar", lambda e, c=c: e.activation(out=sq[c].ap, in_=xc(c, T).ap, func=AF.Square),
                 reads=[xc(c, T)], writes=[sq[c]])
        bk = stats_sum(sq, T)
        rstd = tp(0, T)
        rstd_from(bk, T, 1.0 / D, RMS_EPS, rstd)
        for c in range(16):
            t = tp(1 + (c % 3), T)
            P.op("vector", lambda e, c=c, t=t: e.scalar_tensor_tensor(out=t.ap, in0=xc(c, T).ap, scalar=prmv(0, l, s, c).ap,
                                                                    in1=rstd.ap, op0=ALU.mult, op1=ALU.mult),
                 reads=[xc(c, T), prmv(0, l, s, c), rstd], writes=[t])
            P.op("scalar", lambda e, c=c, t=t: e.activation(out=nc_(c, T).ap, in_=t.ap, func=AF.Identity,
                                                            bias=prmv(1, l, s, c).ap, scale=1.0),
                 reads=[t, prmv(1, l, s, c)], writes=[nc_(c, T)])

    def postnorm(l, s, T):
        sq = [nc_(c, T) for c in range(16)]
        bk = stats_sum(sq, T)
        rstd = tp(0, T)
        rstd_from(bk, T, 1.0 / D, RMS_EPS, rstd)
        for g in range(4):
            yv = MS.v3(4 * g * TM, 4, TM, sub=(0, T))
            xv = xT.v3(4 * g * TM, 4, TM, sub=(0, T))
            P.op("vector", lambda e, yv=yv: e.tensor_tensor(out=yv.ap, in0=yv.ap, in1=rstd.ap.unsqueeze(1).broadcast_to([128, 4, T]), op=ALU.mult),
                 reads=[yv, rstd], writes=[yv])
            P.op("vector", lambda e, yv=yv, xv=xv: e.tensor_tensor(out=xv.ap, in0=xv.ap, in1=yv.ap, op=ALU.add),
                 reads=[yv, xv], writes=[xv])

    def evac_y(bk, dc, T, l, s):
        bv = prmv(2, l, s, dc)
        P.op("scalar", lambda e: e.activation(out=yc(dc, T).ap, in_=bk.t[:, 0:T], func=AF.Identity, scale=bv.ap),
             reads=[bk.v(0, 512), bv], writes=[yc(dc, T)])
        P.op("scalar", lambda e: e.activation(out=nc_(dc, T).ap, in_=bk.t[:, 0:T], func=AF.Square),
             reads=[bk.v(0, 512)], writes=[nc_(dc, T)])

    def tile_base(l):
        return l * TILES_PER_LAYER

    def ffn(l, s, f, T):
        prenorm(l, s, T)
        tb = tile_base(l) + (0 if f == 0 else 34)
        nall = [nc_(c, T) for c in range(16)]
        for g in range(22):
            slot = getw(tb + g, lambda st, g=g: [
                (slot3(st, 16, 512)[:, :, 0:256], kc_view(win_d[l, f], g * 256, 256)),
                (slot3(st, 16, 512)[:, :, 256:512], kc_view(win_d[l, f], FF + g * 256, 256))])
            for h in range(2):
                j = 2 * g + h
                bg, bu = bank(), bank()
                mm_group(bg, T, [(slot.t[:, kc * 512 + h * 128:kc * 512 + h * 128 + 128], nall[kc].ap) for kc in range(16)],
                         reads=[slot.v(0, SLOT)] + nall,
                         fine=([[slot.v(0, SLOT), nall[kc]] for kc in range(16)] if (g == 0 and h == 0) else None))
                mm_group(bu, T, [(slot.t[:, kc * 512 + 256 + h * 128:kc * 512 + 256 + h * 128 + 128], nall[kc].ap) for kc in range(16)],
                         reads=[slot.v(0, SLOT)] + nall)
                st_ = tp(4 + (j % 3), T)
                P.op("scalar", lambda e, bg=bg, st_=st_: e.activation(out=st_.ap, in_=bg.t[:, 0:T], func=AF.Silu),
                     reads=[bg.v(0, 512)], writes=[st_])
                P.op("vector", lambda e, bu=bu, st_=st_, j=j: e.tensor_tensor(out=hc(j, T).ap, in0=st_.ap, in1=bu.t[:, 0:T], op=ALU.mult),
                     reads=[bu.v(0, 512), st_], writes=[hc(j, T)])
        jt_sizes = (16, 16, 12)
        for q in range(4):
            bks = [bank() for _ in range(4)]
            j0 = 0
            for jt, nj in enumerate(jt_sizes):
                slot = getw(tb + 22 + q * 3 + jt, lambda st, q=q, j0=j0, nj=nj: [
                    (slot3(st, nj, 512), wout_d[l, f][j0 * 128:(j0 + nj) * 128, q * 512:(q + 1) * 512].rearrange("(j p) c -> p j c", p=128))])

                def fn(e, slot=slot, bks=bks, j0=j0, nj=nj):
                    ins = None
                    for jj in range(nj):
                        j = j0 + jj
                        for i in range(4):
                            ins = e.matmul(bks[i].t[:, 0:T], lhsT=slot.t[:, jj * 512 + i * 128:jj * 512 + i * 128 + 128],
                                           rhs=hT.t[:, j * TM:j * TM + T], start=(j == 0), stop=(j == 43))
                    return ins
                P.op("tensor", fn, reads=[slot.v(0, SLOT)] + [hc(j0 + jj, T) for jj in range(nj)], writes=[b.v(0, 512) for b in bks])
                j0 += nj
            for i in range(4):
                evac_y(bks[i], 4 * q + i, T, l, s)
        postnorm(l, s, T)

    A_BUF, ACC, PBUF, PT0, PT1, UB, VB, GB, CH = 0, 1656, 3192, 4792, 5192, 5592, 7128, 8664, 10200
    ACC2 = PBUF

    def mixer(l, T, blk):
        NT = T // 128
        prenorm(l, 1, T)
        tb = tile_base(l) + 68
        nall = [nc_(c, T) for c in range(16)]
        AW, PW, CW = 30 + TM, 16 + TM, 2 + TM

        def abuf(c, lo, hi):
            return MS.v(A_BUF + c * AW + lo, A_BUF + c * AW + hi)

        def pbuf(c, lo, hi):
            return MS.v(PBUF + c * PW + lo, PBUF + c * PW + hi)

        def chb(c, lo, hi):
            return MS.v(CH + c * CW + lo, CH + c * CW + hi)

        def acc(c):
            return MS.v(ACC + c * TM, ACC + c * TM + T)

        def ub(c):
            return MS.v(UB + c * TM, UB + c * TM + T)

        def vb(c):
            return MS.v(VB + c * TM, VB + c * TM + T)

        def gb(c):
            return MS.v(GB + c * TM, GB + c * TM + T)

        def cstv(c, o, n):
            b = (l * 4 + c) * 48 + o
            return cst.v(b, b + n)

        for c in range(4):
            P.op("vector", lambda e, c=c: e.tensor_copy(out=abuf(c, 0, 30).ap, in_=cstv(c, 0, 30).ap), reads=[cstv(c, 0, 30)], writes=[abuf(c, 0, 30)])
            P.op("vector", lambda e, c=c: e.tensor_copy(out=pbuf(c, 0, 16).ap, in_=cstv(c, 30, 16).ap), reads=[cstv(c, 30, 16)], writes=[pbuf(c, 0, 16)])
            P.op("vector", lambda e, c=c: e.tensor_copy(out=chb(c, 0, 2).ap, in_=cstv(c, 46, 2).ap), reads=[cstv(c, 46, 2)], writes=[chb(c, 0, 2)])

        def proj_tile(mt):
            slot = getw(tb + mt, lambda st, mt=mt: [(slot3(st, 16, 512), kc_view(mixw_d[l], mt * 512, 512))])
            for i in range(4):
                c = i
                bk = bank()
                mm_group(bk, T, [(slot.t[:, kc * 512 + i * 128:kc * 512 + i * 128 + 128], nall[kc].ap) for kc in range(16)],
                         reads=[slot.v(0, SLOT)] + nall,
                         fine=([[slot.v(0, SLOT), nall[kc]] for kc in range(16)] if (mt == 0 and i == 0) else None))
                src = bk.v(0, 512)
                if mt == 0:
                    d = abuf(c, 30, 30 + T)
                    P.op("scalar", lambda e, bk=bk, d=d: e.copy(out=d.ap, in_=bk.t[:, 0:T]), reads=[src], writes=[d])
                elif mt == 1:
                    t = tp(4 + (i % 3), T)
                    d = abuf(c, 30, 30 + T)
                    P.op("scalar", lambda e, bk=bk, t=t: e.activation(out=t.ap, in_=bk.t[:, 0:T], func=AF.Sigmoid), reads=[src], writes=[t])
                    P.op("vector", lambda e, t=t, d=d: e.tensor_tensor(out=d.ap, in0=d.ap, in1=t.ap, op=ALU.mult), reads=[t, d], writes=[d])
                elif mt == 2:
                    d = pbuf(c, 16, 16 + T)
                    P.op("scalar", lambda e, bk=bk, d=d: e.copy(out=d.ap, in_=bk.t[:, 0:T]), reads=[src], writes=[d])
                elif mt == 3:
                    d = ub(c)
                    P.op("scalar", lambda e, bk=bk, d=d: e.activation(out=d.ap, in_=bk.t[:, 0:T], func=AF.Gelu), reads=[src], writes=[d])
                elif mt == 4:
                    d = vb(c)
                    P.op("scalar", lambda e, bk=bk, d=d: e.activation(out=d.ap, in_=bk.t[:, 0:T], func=AF.Gelu), reads=[src], writes=[d])
                elif mt == 5:
                    d = gb(c)
                    P.op("scalar", lambda e, bk=bk, d=d: e.copy(out=d.ap, in_=bk.t[:, 0:T]), reads=[src], writes=[d])
                elif mt == 6:
                    d = chb(c, 2, 2 + T)
                    P.op("scalar", lambda e, bk=bk, d=d: e.copy(out=d.ap, in_=bk.t[:, 0:T]), reads=[src], writes=[d])
                else:
                    d = chb(c, 2, 2 + T)
                    P.op("vector", lambda e, bk=bk, d=d: e.tensor_tensor(out=d.ap, in0=d.ap, in1=bk.t[:, 0:T], op=ALU.mult), reads=[src, d], writes=[d])

        mk = cc("mask", 0)

        def mask_and_save(bufs_fn, so, sn):
            if blk == 0:
                for c in range(4):
                    d = bufs_fn(c, HALO)
                    P.op("vector", lambda e, d=d: e.tensor_scalar(out=d.ap, in0=d.ap, scalar1=mk.ap, scalar2=None, op0=ALU.mult),
                         reads=[d, mk], writes=[d])
            for c in range(4):
                src_ = bufs_fn(c, None)
                P.op("vector", lambda e, c=c, src_=src_: e.tensor_copy(out=cstv(c, so, sn).ap, in_=src_.ap), reads=[src_], writes=[cstv(c, so, sn)])

        for mt in (0, 1):
            proj_tile(mt)
        mask_and_save(lambda c, h: abuf(c, 30, 30 + h) if h else abuf(c, T, T + 30), 0, 30)

        def cw(c, k):
            return cc("conv_w", (l * 4 + c) * 31 + k)

        for c in range(4):
            P.op("vector", lambda e, c=c: e.tensor_scalar(out=acc(c).ap, in0=abuf(c, 0, T).ap, scalar1=cw(c, 0).ap,
                                                          scalar2=cc("conv_b", l * 4 + c).ap, op0=ALU.mult, op1=ALU.add),
                 reads=[abuf(c, 0, T), cw(c, 0), cc("conv_b", l * 4 + c)], writes=[acc(c)])
        for k in range(1, 31):
            for c in range(4):
                P.op("vector", lambda e, c=c, k=k: e.scalar_tensor_tensor(out=acc(c).ap, in0=abuf(c, k, k + T).ap, scalar=cw(c, k).ap,
                                                                          in1=acc(c).ap, op0=ALU.mult, op1=ALU.add),
                     reads=[abuf(c, k, k + T), cw(c, k), acc(c)], writes=[acc(c)])
        for mt in (6, 5, 2, 3, 4, 7):
            proj_tile(mt)
        mask_and_save(lambda c, h: pbuf(c, 16, 16 + h) if h else pbuf(c, T, T + 16), 30, 16)
        mask_and_save(lambda c, h: chb(c, 2, 2 + h) if h else chb(c, T, T + 2), 46, 2)

        def ln_stats(src, c0, mean, rstd):
            sb = [hc(c0 + c, T) for c in range(4)]
            sqb = [hc(c0 + 4 + c, T) for c in range(4)]
            for c in range(4):
                P.op("scalar", lambda e, c=c: e.copy(out=sb[c].ap, in_=src(c).ap), reads=[src(c)], writes=[sb[c]])
                P.op("scalar", lambda e, c=c: e.activation(out=sqb[c].ap, in_=src(c).ap, func=AF.Square), reads=[src(c)], writes=[sqb[c]])
            bm, bq = stats_sum(sb, T), stats_sum(sqb, T)
            msq = tp(1, T)
            P.op("vector", lambda e: e.tensor_scalar(out=mean.ap, in0=bm.t[:, 0:T], scalar1=1.0 / BW, scalar2=None, op0=ALU.mult),
                 reads=[bm.v(0, 512)], writes=[mean])
            P.op("vector", lambda e: e.tensor_tensor(out=msq.ap, in0=mean.ap, in1=mean.ap, op=ALU.mult), reads=[mean], writes=[msq])
            rstd_from(bq, T, 1.0 / BW, LN_EPS, rstd, pre_sub=msq)

        def ln_apply(src, mean, rstd, gname, bname, out_fn, act):
            for c in range(4):
                t = tp(4 + (c % 3), T)
                P.op("vector", lambda e, c=c, t=t: e.tensor_tensor(out=t.ap, in0=src(c).ap, in1=mean.ap, op=ALU.subtract),
                     reads=[src(c), mean], writes=[t])
                P.op("vector", lambda e, t=t: e.tensor_tensor(out=t.ap, in0=t.ap, in1=rstd.ap, op=ALU.mult), reads=[t, rstd], writes=[t])
                o = out_fn(c)
                P.op("scalar", lambda e, c=c, t=t, o=o: e.activation(out=o.ap, in_=t.ap, func=act, bias=cc(bname, l * 4 + c).ap,
                                                                     scale=cc(gname, l * 4 + c).ap),
                     reads=[t, cc(bname, l * 4 + c), cc(gname, l * 4 + c)], writes=[o])

        sgu_mean, sgu_rstd = tp(7, T), tp(3, T)
        conv_mean, conv_rstd = tp(0, T), tp(2, T)
        ln_stats(vb, 24, sgu_mean, sgu_rstd)
        ln_stats(acc, 16, conv_mean, conv_rstd)

        def pool_branch():
            pt = [MS.v(PT0, PT0 + PW), MS.v(PT1, PT1 + PW)]
            for g, w in enumerate((2, 4, 8, 16)):
                cur = MS.v(PBUF + g * PW, PBUF + (g + 1) * PW)
                curo = PBUF + g * PW
                sh = 1
                k = 0
                while sh < w:
                    dst = pt[k % 2]
                    dsto = PT0 if k % 2 == 0 else PT1
                    lo = 2 * sh - 1
                    P.op("vector", lambda e, curo=curo, dsto=dsto, sh=sh, lo=lo: e.tensor_tensor(
                        out=MS.t[:, dsto + lo:dsto + 16 + T], in0=MS.t[:, curo + lo:curo + 16 + T],
                        in1=MS.t[:, curo + lo - sh:curo + 16 + T - sh], op=ALU.add),
                        reads=[cur], writes=[dst])
                    cur, curo = dst, dsto
                    sh *= 2
                    k += 1
                pin = pbuf(g, 16, 16 + T)
                pooled = hc(32 + g, T)
                if blk == 0:
                    q = tp(1, T)
                    P.op("vector", lambda e, curo=curo, q=q, w=w: e.tensor_scalar(out=q.ap, in0=MS.t[:, curo + 16:curo + 16 + T], scalar1=1.0 / w,
                                                                                 scalar2=None, op0=ALU.mult), reads=[cur], writes=[q])
                    co = CPP_OFF["corr"] + g * 16
                    P.op("vector", lambda e, q=q, co=co: e.tensor_tensor(out=q.ap[:, HALO:HALO + 16], in0=q.ap[:, HALO:HALO + 16],
                                                                        in1=cpp.t[:, co:co + 16], op=ALU.mult),
                         reads=[q, cpp.v(co, co + 16)], writes=[q])
                    P.op("vector", lambda e, q=q, pin=pin, pooled=pooled: e.tensor_tensor(out=pooled.ap, in0=q.ap, in1=pin.ap, op=ALU.subtract),
                         reads=[q, pin], writes=[pooled])
                else:
                    P.op("vector", lambda e, curo=curo, pin=pin, pooled=pooled, w=w: e.scalar_tensor_tensor(
                        out=pooled.ap, in0=MS.t[:, curo + 16:curo + 16 + T], scalar=1.0 / w, in1=pin.ap, op0=ALU.mult, op1=ALU.subtract),
                        reads=[cur, pin], writes=[pooled])
                bk = bank()
                mm_group(bk, T, [(poolw.t[:, l * 512 + g * 128:l * 512 + (g + 1) * 128], pooled.ap)],
                         reads=[poolw.v(l * 512, (l + 1) * 512), pooled])
                o = hc(4 + g, T)
                P.op("scalar", lambda e, bk=bk, o=o, g=g: e.activation(out=o.ap, in_=bk.t[:, 0:T], func=AF.Identity,
                                                                       scale=cc("pool_scale", l * 4 + g).ap),
                     reads=[bk.v(0, 512), cc("pool_scale", l * 4 + g)], writes=[o])


        def sgu_branch():
            ln_apply(vb, sgu_mean, sgu_rstd, "sgu_ln_g", "sgu_ln_b", lambda c: hc(36 + c, T), AF.Identity)
            vtok0 = 40 * TM
            npair = NT * 4
            done = 0
            while done < npair:
                cnt = min(8, npair - done)
                srcs = []
                for idx in range(done, done + cnt):
                    n, g = divmod(idx, 4)
                    srcs.append(hT.v((36 + g) * TM + n * 128, (36 + g) * TM + (n + 1) * 128))

                def fn(e, srcs=srcs):
                    ins = None
                    for i, s_ in enumerate(srcs):
                        ins = e.transpose(out=psb.t[:, i * 128:(i + 1) * 128], in_=s_.ap, identity=identb.t[:, 0:128])
                    return ins
                P.op("tensor", fn, reads=srcs + [identb.v(0, 128)], writes=[psb.v(0, 1024)])
                dst = hT.v(vtok0 + done * 128, vtok0 + (done + cnt) * 128)
                P.op("scalar", lambda e, dst=dst, cnt=cnt: e.copy(out=dst.ap, in_=psb.t[:, 0:cnt * 128]), reads=[psb.v(0, 1024)], writes=[dst])
                done += cnt
            for g in range(4):
                bk = bank()

                def fn(e, bk=bk, g=g):
                    ins = None
                    for n in range(NT):
                        o = vtok0 + (n * 4 + g) * 128
                        ins = e.matmul(bk.t[:, n * 128:(n + 1) * 128], lhsT=hT.t[:, o:o + 128],
                                       rhs=sguw.t[:, l * 512 + g * 128:l * 512 + (g + 1) * 128], start=True, stop=True)
                    return ins
                P.op("tensor", fn, reads=[hT.v(vtok0, vtok0 + npair * 128), sguw.v(l * 512, (l + 1) * 512)], writes=[bk.v(0, 512)])
                t = tp(4 + (g % 3), T)
                bo = l * 512 + g * 128
                P.op("vector", lambda e, bk=bk, t=t, bo=bo: e.tensor_tensor(
                    out=t.ap.rearrange("p (n t) -> p n t", n=NT), in0=bk.t[:, 0:T].rearrange("p (n t) -> p n t", n=NT),
                    in1=bsb.t[:, bo:bo + 128].unsqueeze(1).broadcast_to([128, NT, 128]), op=ALU.add),
                    reads=[bk.v(0, 512), bsb.v(bo, bo + 128)], writes=[t])
                o = hc(8 + g, T)
                P.op("vector", lambda e, t=t, o=o, g=g: e.tensor_tensor(out=o.ap, in0=t.ap, in1=ub(g).ap, op=ALU.mult),
                     reads=[t, ub(g)], writes=[o])


        def conv_ln_branch():
            ln_apply(acc, conv_mean, conv_rstd, "conv_ln_g", "conv_ln_b", lambda c: hc(c, T), AF.Silu)


        def sconv_branch():
            def sw(c, k):
                return cc("sconv_w", (l * 4 + c) * 3 + k)

            for c in range(4):
                a2 = acc(c)
                P.op("vector", lambda e, c=c, a2=a2: e.tensor_scalar(out=a2.ap, in0=chb(c, 0, T).ap, scalar1=sw(c, 0).ap, scalar2=None, op0=ALU.mult),
                     reads=[chb(c, 0, T), sw(c, 0)], writes=[a2])
            for k in (1, 2):
                for c in range(4):
                    a2 = acc(c)
                    P.op("vector", lambda e, c=c, k=k, a2=a2: e.scalar_tensor_tensor(out=a2.ap, in0=chb(c, k, k + T).ap, scalar=sw(c, k).ap,
                                                                                    in1=a2.ap, op0=ALU.mult, op1=ALU.add),
                         reads=[chb(c, k, k + T), sw(c, k), a2], writes=[a2])
            for c in range(4):
                o = hc(12 + c, T)
                P.op("vector", lambda e, c=c, o=o: e.tensor_tensor(out=o.ap, in0=acc(c).ap, in1=gb(c).ap, op=ALU.mult),
                     reads=[acc(c), gb(c)], writes=[o])


        yb = [hc(c, T) for c in range(16)]
        BORDER = (1, 2, 0, 3)

        def macc(i):
            return MS.v(ACC2 + i * TM, ACC2 + i * TM + T)

        def gate_step(qd, b):
            first, last = (b == BORDER[0]), (b == BORDER[-1])
            brs = getbr(l, tb + 8 + qd * 5, qd, b)
            gs = getw(tb + 8 + qd * 5 + 1 + b, lambda st, qd=qd, b=b: [
                (slot3(st, 16, 512), kc_view(gatew_d[l, b], qd * 512, 512))])
            for i in range(4):
                dc = 4 * qd + i
                bg, by = bank(), bank()
                mm_group(bg, T, [(gs.t[:, kc * 512 + i * 128:kc * 512 + i * 128 + 128], nall[kc].ap) for kc in range(16)],
                         reads=[gs.v(0, SLOT)] + nall)
                mm_group(by, T, [(brs.t[:, c * 512 + i * 128:c * 512 + i * 128 + 128], yb[b * 4 + c].ap) for c in range(4)],
                         reads=[brs.v(0, 2048)] + yb[b * 4:b * 4 + 4])
                sg = tp(4 + (i % 2), T)
                gbv = cc("gate_b", (l * 4 + b) * 16 + dc)
                P.op("scalar", lambda e, bg=bg, sg=sg, gbv=gbv: e.activation(out=sg.ap, in_=bg.t[:, 0:T], func=AF.Sigmoid, bias=gbv.ap, scale=1.0),
                     reads=[bg.v(0, 512), gbv], writes=[sg])
                if first:
                    P.op("vector", lambda e, by=by, sg=sg, i=i: e.tensor_tensor(out=macc(i).ap, in0=sg.ap, in1=by.t[:, 0:T], op=ALU.mult),
                         reads=[by.v(0, 512), sg], writes=[macc(i)])
                else:
                    t = tp(6 if (i % 2) else 1, T)
                    P.op("vector", lambda e, by=by, sg=sg, t=t: e.tensor_tensor(out=t.ap, in0=sg.ap, in1=by.t[:, 0:T], op=ALU.mult),
                         reads=[by.v(0, 512), sg], writes=[t])
                    o = hc(16 + dc, T) if last else macc(i)
                    P.op("vector", lambda e, t=t, o=o, i=i: e.tensor_tensor(out=o.ap, in0=macc(i).ap, in1=t.ap, op=ALU.add),
                         reads=[t, macc(i)], writes=[o])

        if debug and l == 0 and blk == 0:
            pool_branch(); sgu_branch(); conv_ln_branch(); sconv_branch()
            P.dma("sync", [(dbg_d, hT.t[:, 0:16 * TM])], dsem["ld"], reads=[hT.v(0, 16 * TM)])
            for qd in range(4):
                for b in BORDER:
                    gate_step(qd, b)
        else:
            pool_branch()
            gate_step(0, 1)
            sgu_branch()
            gate_step(0, 2)
            conv_ln_branch()
            gate_step(0, 0)
            sconv_branch()
            gate_step(0, 3)
            for qd in range(1, 4):
                for b in BORDER:
                    gate_step(qd, b)

        mg = [hc(16 + c, T) for c in range(16)]
        if debug and l == 0 and blk == 0:
            P.dma("sync", [(dbg2_d, hT.t[:, 16 * TM:32 * TM])], dsem["ld"], reads=[hT.v(16 * TM, 32 * TM)])
        for qd in range(4):
            ws_ = getw(tb + 28 + qd, lambda st, qd=qd: [(slot3(st, 16, 512), kc_view(wo_d[l], qd * 512, 512))])
            for i in range(4):
                dc = 4 * qd + i
                bk = bank()
                mm_group(bk, T, [(ws_.t[:, kc * 512 + i * 128:kc * 512 + i * 128 + 128], mg[kc].ap) for kc in range(16)],
                         reads=[ws_.v(0, SLOT)] + mg)
                evac_y(bk, dc, T, l, 1)
        postnorm(l, 1, T)

    XS = 6144

    def load_x(eng, t_start, Tn_):
        nt = Tn_ // 128
        P.dma(eng, [(MS.t[:, XS + n * 2048:XS + (n + 1) * 2048], x_d[t_start + n * 128:t_start + (n + 1) * 128, :]) for n in range(nt)],
              dsem["xin"], writes=[MS.v(XS, XS + nt * 2048)])

    load_x("sync", 0, blocks[0])
    prologue_ada()
    t0 = 0
    for blk, T in enumerate(blocks):
        NT = T // 128
        wstate["blk"] = blk
        if blk > 0 and blk % 2 == 0:
            P.new_epoch()
        for n in range(NT):
            for q in range(4):
                bk = bank()

                def fn(e, bk=bk, n=n, q=q):
                    ins = None
                    for j in range(4):
                        dc = 4 * q + j
                        ins = e.transpose(out=bk.t[:, j * 128:(j + 1) * 128], in_=MS.t[:, XS + n * 2048 + dc * 128:XS + n * 2048 + (dc + 1) * 128],
                                          identity=ident.t[:, 0:128])
                    return ins
                P.op("tensor", fn, reads=[MS.v(XS + n * 2048, XS + (n + 1) * 2048), ident.v(0, 128)], writes=[bk.v(0, 512)])
                d = xT.v3(4 * q * TM, 4, TM, sub=(n * 128, (n + 1) * 128))
                P.op("vector" if (q % 2) else "scalar",
                     (lambda e, bk=bk, d=d: e.tensor_copy(out=d.ap, in_=bk.t[:, 0:512].rearrange("p (a b) -> p a b", a=4))) if (q % 2) else
                     (lambda e, bk=bk, d=d: e.copy(out=d.ap, in_=bk.t[:, 0:512].rearrange("p (a b) -> p a b", a=4))),
                     reads=[bk.v(0, 512)], writes=[d])
        prefetched = False
        for l in range(n_layers):
            if "ffn0" in stages:
                ffn(l, 0, 0, T)
            if "mix" in stages:
                mixer(l, T, blk)
            if l == n_layers - 1 and blk + 1 < len(blocks):
                load_x("scalar", t0 + T, blocks[blk + 1])
                prefetched = True
            if "ffn1" in stages:
                ffn(l, 2, 1, T)
        n_first = 1 if blk == 0 else 0
        stg = MS.v(0, NT * 2048)
        for n in range(n_first, NT):
            for q in range(4):
                bk = bank()

                def fn(e, bk=bk, n=n, q=q):
                    ins = None
                    for j in range(4):
                        dc = 4 * q + j
                        ins = e.transpose(out=bk.t[:, j * 128:(j + 1) * 128], in_=xT.t[:, dc * TM + n * 128:dc * TM + (n + 1) * 128],
                                          identity=ident.t[:, 0:128])
                    return ins
                P.op("tensor", fn, reads=[xT.v(4 * q * TM, (4 * q + 4) * TM), ident.v(0, 128)], writes=[bk.v(0, 512)])
                d = MS.v(n * 2048 + q * 512, n * 2048 + (q + 1) * 512)
                P.op("scalar", lambda e, bk=bk, d=d: e.copy(out=d.ap, in_=bk.t[:, 0:512]), reads=[bk.v(0, 512)], writes=[d])
        if NT > n_first:
            tok = P.dma("sync", [(out_d[t0 - HALO + n * 128:t0 - HALO + (n + 1) * 128, :], MS.t[:, n * 2048:(n + 1) * 2048])
                                 for n in range(n_first, NT)], dsem["xout"], reads=[MS.v(n_first * 2048, NT * 2048)])
            last_tok = tok
        t0 += T
    P.wait_on("sync", last_tok)
    P.emit()
    return nc


BLOCKS_FULL = [384, 384, 384, 384, 384, 256]


def _host_inputs(inp, core, blocks):
    ntok = sum(blocks)
    x = np.asarray(inp["x"], np.float32)[0]
    g0 = core * TOK_PER_CORE - HALO
    xs = np.zeros((ntok, D), np.float32)
    lo = max(g0, 0)
    xs[lo - g0:, :] = x[lo:g0 + ntok]
    f = lambda k: np.ascontiguousarray(np.asarray(inp[k], np.float32))
    m = {
        "x": xs,
        "cpp": _build_cpp(inp, core),
        "bsb": np.ascontiguousarray(np.broadcast_to(np.asarray(inp["sgu_b_s"], np.float32).reshape(1, NL * 512), (128, NL * 512))),
        "ident": np.eye(128, dtype=np.float32),
        "cmask": np.triu(np.ones((128, 128), np.float32)),
        "ada_w": f("ada_w"), "ffn_w_in": f("ffn_w_in"), "ffn_w_out": f("ffn_w_out"), "mix_w_in": f("mix_w_in"),
        "gate_w": f("gate_w"), "branch_w_out": f("branch_w_out"), "w_o": f("w_o"), "pool_group_w": f("pool_group_w"),
        "sgu_w_sT": np.ascontiguousarray(np.asarray(inp["sgu_w_s"], np.float32).transpose(0, 1, 3, 2)),
    }
    return m


def kernel(**inp):
    blocks = BLOCKS_FULL
    nc = build(blocks)
    in_maps = [_host_inputs(inp, c, blocks) for c in range(NCORES)]
    res = run_bass_kernel_spmd(nc, in_maps, core_ids=list(range(NCORES)))
    out = np.concatenate([np.asarray(r["out"], np.float32) for r in res.results], axis=0)
    return out.reshape(1, NCORES * TOK_PER_CORE, D)
```

```python
import numpy as np
import concourse.bass as bass
import concourse.mybir as mybir
from concourse.bass_utils import run_bass_kernel_spmd

F32 = mybir.dt.float32
BF16 = mybir.dt.bfloat16
AF = mybir.ActivationFunctionType
ALU = mybir.AluOpType
AX = mybir.AxisListType

ENGS = ("tensor", "scalar", "vector", "gpsimd", "sync")

D = 2048
FF = 5632
BW = 512
NL = 2
NCORES = 8
TOK_PER_CORE = 2048
HALO = 128
TMAX = 384
RMS_EPS = 1e-6
LN_EPS = 1e-5
SLOT = 8192
NSLOT = 3


class V:
    __slots__ = ("ap", "key")

    def __init__(self, ap, key):
        self.ap = ap
        self.key = key


class Plan:
    def __init__(self, nc):
        self.nc = nc
        self.ops = {e: [] for e in ENGS}
        self.semname = {}
        self.cnt = {}
        self.seen = {e: {} for e in ENGS}
        self.hist = {}
        self.recs = {}
        self.handles = {}
        self.nsem = 0
        for e in ENGS:
            self._new_eng_sem(e)

    def new_sem(self, name):
        h = self.nc.alloc_semaphore(name)
        self.handles[name] = h
        self.cnt[name] = 0
        self.nsem += 1
        return name

    def _new_eng_sem(self, e):
        name = f"p_{e}_{self.nsem}"
        self.new_sem(name)
        self.semname[e] = name

    def new_epoch(self):
        for e in ENGS:
            self._new_eng_sem(e)

    def _ov(self, key):
        name, lo, hi = key
        lst = self.recs.setdefault(name, [])
        return lst, [r for r in lst if r[0] < hi and lo < r[1]]

    def _deps_for(self, reads, writes):
        deps = {}

        def add(s, v):
            if deps.get(s, 0) < v:
                deps[s] = v

        for k in reads:
            for r in self._ov(k)[1]:
                if r[2] is not None:
                    add(*r[2])
        for k in writes:
            for r in self._ov(k)[1]:
                if r[2] is not None:
                    add(*r[2])
                for s, v in r[3].items():
                    add(s, v)
        return deps

    def _record(self, reads, writes, tok):
        s, v = tok
        for k in reads:
            lst, ov = self._ov(k)
            exact = None
            for r in ov:
                if r[0] == k[1] and r[1] == k[2]:
                    exact = r
            if exact is None:
                exact = [k[1], k[2], None, {}]
                lst.append(exact)
            if exact[3].get(s, 0) < v:
                exact[3][s] = v
        for k in writes:
            lst, ov = self._ov(k)
            for r in ov:
                if k[1] <= r[0] and r[1] <= k[2]:
                    lst.remove(r)
            lst.append([k[1], k[2], (s, v), {}])

    def _waits(self, eng, deps, skip_self=False):
        waits = []
        seen = self.seen[eng]
        own = self.semname[eng]
        for s, v in deps.items():
            if skip_self and s == own:
                continue
            if seen.get(s, 0) >= v:
                if s != own:
                    continue
                if seen.get(("own", s), 0) >= v:
                    continue
            waits.append((s, v))
            if s == own:
                seen[("own", s)] = v
            if seen.get(s, 0) < v:
                seen[s] = v
            h = self.hist.get((s, v))
            if h:
                for s2, v2 in h.items():
                    if seen.get(s2, 0) < v2:
                        seen[s2] = v2
        return waits

    @staticmethod
    def _keys(lst):
        return [x.key if isinstance(x, V) else x for x in lst]

    def op(self, eng, fn, reads=(), writes=()):
        rk = self._keys(reads)
        wk = self._keys(writes)
        wk = wk + [k for k in rk if k[0].startswith("ps")]
        rk = [k for k in rk if not k[0].startswith("ps")]
        deps = self._deps_for(rk, wk)
        waits = self._waits(eng, deps, skip_self=(eng == "tensor"))
        s = self.semname[eng]
        self.cnt[s] += 1
        v = self.cnt[s]
        self.seen[eng][s] = v
        self.hist[(s, v)] = {k: x for k, x in self.seen[eng].items() if not isinstance(k, tuple)}
        self.ops[eng].append((waits, fn, (s, 1)))
        self._record(rk, wk, (s, v))
        return (s, v)

    def dma(self, eng, pairs, dsem, reads=(), writes=()):
        rk = self._keys(reads)
        wk = self._keys(writes)
        deps = self._deps_for(rk, wk)
        if self.cnt[dsem] > 0 and deps.get(dsem, 0) < self.cnt[dsem]:
            deps[dsem] = self.cnt[dsem]
        waits = self._waits(eng, deps)
        self.cnt[dsem] += 16 * len(pairs)
        v = self.cnt[dsem]
        self.hist[(dsem, v)] = {k: x for k, x in self.seen[eng].items() if not isinstance(k, tuple)}
        h = self.handles[dsem]

        def fn(e, pairs=pairs, h=h):
            for (o, i) in pairs:
                e.dma_start(out=o, in_=i).then_inc(h, 16)
            return None

        self.ops[eng].append((waits, fn, None))
        self._record(rk, wk, (dsem, v))
        return (dsem, v)

    def wait_on(self, eng, tok):
        waits = self._waits(eng, {tok[0]: tok[1]})
        if waits:
            self.ops[eng].append((waits, None, None))

    def emit(self):
        nc = self.nc
        handles = self.handles
        ops = self.ops
        with nc.Block() as block:
            def run(e, lst):
                for waits, fn, inc in lst:
                    for s, v in waits:
                        e.wait_ge(handles[s], v)
                    if fn is None:
                        continue
                    ins = fn(e)
                    if inc is not None:
                        ins.then_inc(handles[inc[0]], inc[1])

            @block.tensor
            def _(e):
                run(e, ops["tensor"])

            @block.scalar
            def _(e):
                run(e, ops["scalar"])

            @block.vector
            def _(e):
                run(e, ops["vector"])

            @block.gpsimd
            def _(e):
                run(e, ops["gpsimd"])

            @block.sync
            def _(e):
                run(e, ops["sync"])


class Tn:
    def __init__(self, nc, name, cols, dtype, psum=False, parts=128):
        self.name = name
        self.cols = cols
        if psum:
            self.t = nc.alloc_psum_tensor(name, [parts, cols], dtype)
        else:
            self.t = nc.alloc_sbuf_tensor("sb_" + name, [parts, cols], dtype)

    def v(self, lo, hi):
        return V(self.t[:, lo:hi], (self.name, lo, hi))

    def v3(self, lo, a, b, sub=None):
        ap = self.t[:, lo:lo + a * b].rearrange("p (a b) -> p a b", a=a)
        if sub is not None:
            ap = ap[:, :, sub[0]:sub[1]]
            return V(ap, (self.name, lo + sub[0], lo + (a - 1) * b + sub[1]))
        return V(ap, (self.name, lo, lo + a * b))


def _cpp_layout():
    off = {}
    n = 0

    def add(name, cols):
        nonlocal n
        off[name] = n
        n += cols

    add("c", 16)
    add("ada_b", NL * 144)
    add("pre_g", NL * 3 * 16)
    add("post_g", NL * 3 * 16)
    add("gate_b", NL * 4 * 16)
    add("conv_w", NL * 4 * 31)
    add("conv_b", NL * 4)
    add("conv_ln_g", NL * 4)
    add("conv_ln_b", NL * 4)
    add("pool_scale", NL * 4)
    add("sgu_ln_g", NL * 4)
    add("sgu_ln_b", NL * 4)
    add("sconv_w", NL * 4 * 3)
    add("mask", 1)
    add("corr", 4 * 16)
    return off, n


CPP_OFF, CPP_N = _cpp_layout()


def _pp(a):
    a = np.asarray(a, dtype=np.float32)
    lead = a.shape[:-1]
    n = a.shape[-1] // 128
    a = a.reshape(lead + (n, 128))
    a = np.moveaxis(a, -1, 0)
    return np.ascontiguousarray(a.reshape(128, -1))


def _build_cpp(inp, core):
    cpp = np.zeros((128, CPP_N), np.float32)

    def put(name, arr):
        cpp[:, CPP_OFF[name]:CPP_OFF[name] + arr.shape[1]] = arr

    put("c", _pp(inp["c"][0]))
    put("ada_b", _pp(inp["ada_b"]))
    put("pre_g", _pp(inp["pre_g"]))
    put("post_g", _pp(inp["post_g"]))
    put("gate_b", _pp(inp["gate_b"]))
    cw = np.asarray(inp["conv_w"], np.float32).reshape(NL, 31, 4, 128)
    put("conv_w", np.ascontiguousarray(cw.transpose(3, 0, 2, 1).reshape(128, -1)))
    for nm in ("conv_b", "conv_ln_g", "conv_ln_b", "pool_scale", "sgu_ln_g", "sgu_ln_b"):
        put(nm, _pp(inp[nm]))
    sw = np.asarray(inp["sconv_w"], np.float32).reshape(NL, 3, 4, 128)
    put("sconv_w", np.ascontiguousarray(sw.transpose(3, 0, 2, 1).reshape(128, -1)))
    cpp[:, CPP_OFF["mask"]] = 0.0 if core == 0 else 1.0
    corr = np.ones((4, 16), np.float32)
    if core == 0:
        for g, w in enumerate((2, 4, 8, 16)):
            for t in range(16):
                corr[g, t] = float(w) / float(min(t + 1, w))
    cpp[:, CPP_OFF["corr"]:CPP_OFF["corr"] + 64] = corr.reshape(1, 64)
    return cpp


def build(blocks, n_layers=NL, stages=("ffn0", "mix", "ffn1"), scratch=True, debug=False):
    ntok = sum(blocks)
    nc = bass.Bass("TRN2", target_bir_lowering=False)
    P = Plan(nc)

    def din(name, shape, dt=F32):
        return nc.dram_tensor(name, list(shape), dt, kind="ExternalInput").ap()

    x_d = din("x", [128, 16, ntok])
    out_d = nc.dram_tensor("out", [D, ntok - HALO], F32, kind="ExternalOutput").ap()
    cpp_d = din("cpp", [128, CPP_N])
    bsb_d = din("bsb", [128, NL * 512])
    ident_d = din("ident", [128, 128])
    cmask_d = din("cmask", [128, 128])
    ada_w_d = din("ada_w", [NL, D, 18432])
    win_d = din("ffn_w_in", [NL, 2, D, 2 * FF])
    wout_d = din("ffn_w_out", [NL, 2, FF, D])
    mixw_d = din("mix_w_in", [NL, D, 4096])
    gatew_d = din("gate_w", [NL, 4, D, D])
    brw_d = din("branch_w_out", [NL, 4, BW, D])
    wo_d = din("w_o", [NL, D, D])
    poolw_d = din("pool_group_w", [NL, 4, 128, 128])
    sguw_d = din("sgu_w_sT", [NL, 4, 128, 128])

    if debug:
        dbg_d = nc.dram_tensor("dbg", [128, 16 * TMAX], BF16, kind="ExternalOutput").ap()
        dbg2_d = nc.dram_tensor("dbg2", [128, 16 * TMAX], BF16, kind="ExternalOutput").ap()
    TILES_PER_LAYER = 100
    if scratch:
        wsc_l = [nc.dram_tensor(f"wscratch{l}", [TILES_PER_LAYER, 128, SLOT], BF16, kind="Internal").ap() for l in range(NL)]
        wsc_d = {l * TILES_PER_LAYER + i: wsc_l[l][i] for l in range(NL) for i in range(TILES_PER_LAYER)}

    TM = TMAX
    xT = Tn(nc, "xT", 16 * TM, F32)
    MS = Tn(nc, "MS", 12288, F32)
    nT = Tn(nc, "nT", 16 * TM, BF16)
    hT = Tn(nc, "hT", 44 * TM, BF16)
    wring = [Tn(nc, f"w{i}", SLOT, BF16) for i in range(NSLOT)]
    wsem = [P.new_sem(f"ws{i}") for i in range(NSLOT)]
    wbsem = [P.new_sem(f"wb{i}") for i in range(NSLOT)]
    cpp = Tn(nc, "cpp", CPP_N, F32)
    bsb = Tn(nc, "bsb", NL * 512, F32)
    ident = Tn(nc, "ident", 128, F32)
    identb = Tn(nc, "identb", 128, BF16)
    ones = Tn(nc, "ones", 128, BF16)
    cmask = Tn(nc, "cmask", 128, F32)
    poolw = Tn(nc, "poolw", NL * 512, BF16)
    sguw = Tn(nc, "sguw", NL * 512, BF16)
    adar = Tn(nc, "adar", NL * 144, F32)
    cond = Tn(nc, "cond", 16, F32)
    prm = Tn(nc, "prm", 3 * NL * 48, F32)
    TP = Tn(nc, "TP", 8 * TM, F32)
    cst = Tn(nc, "cst", NL * 4 * 48, F32)
    NPS = 7
    ps = [Tn(nc, f"ps{i}", 512, F32, psum=True) for i in range(NPS)]
    psb = Tn(nc, "psb", 1024, BF16, psum=True)
    bank_i = [0]

    def bank():
        b = ps[bank_i[0] % NPS]
        bank_i[0] += 1
        return b

    dsem = {k: P.new_sem(k) for k in ("ld", "xin", "xout")}

    def cc(name, idx, n=1):
        o = CPP_OFF[name] + idx
        return cpp.v(o, o + n)

    def xc(c, T):
        return xT.v(c * TM, c * TM + T)

    def nc_(c, T):
        return nT.v(c * TM, c * TM + T)

    def hc(c, T):
        return hT.v(c * TM, c * TM + T)

    def yc(c, T):
        return MS.v(c * TM, c * TM + T)

    def tp(i, T):
        return TP.v(i * TM, i * TM + T)

    P.dma("sync", [(cpp.t[:], cpp_d), (bsb.t[:], bsb_d), (ident.t[:], ident_d), (cmask.t[:], cmask_d)], dsem["ld"],
          writes=[cpp.v(0, CPP_N), bsb.v(0, NL * 512), ident.v(0, 128), cmask.v(0, 128)])
    P.op("vector", lambda e: e.tensor_copy(out=identb.t[:], in_=ident.t[:]), reads=[ident.v(0, 128)], writes=[identb.v(0, 128)])
    P.op("vector", lambda e: e.memset(ones.t[:], 1.0), writes=[ones.v(0, 128)])
    P.op("vector", lambda e: e.memset(cst.t[:], 0.0), writes=[cst.v(0, NL * 4 * 48)])
    for l in range(n_layers):
        stg = MS.v(0, 1024)
        P.dma("sync", [(MS.t[:, 0:512].rearrange("p (g d) -> p g d", g=4), poolw_d[l].rearrange("g c d -> c g d")),
                       (MS.t[:, 512:1024].rearrange("p (g t) -> p g t", g=4), sguw_d[l].rearrange("g j t -> j g t"))],
              dsem["ld"], writes=[stg])
        P.op("vector", lambda e, l=l: e.tensor_copy(out=poolw.t[:, l * 512:(l + 1) * 512], in_=MS.t[:, 0:512]),
             reads=[MS.v(0, 512)], writes=[poolw.v(l * 512, (l + 1) * 512)])
        P.op("vector", lambda e, l=l: e.tensor_tensor(
            out=sguw.t[:, l * 512:(l + 1) * 512].rearrange("p (g t) -> p g t", g=4),
            in0=MS.t[:, 512:1024].rearrange("p (g t) -> p g t", g=4),
            in1=cmask.t[:, 0:128].unsqueeze(1).broadcast_to([128, 4, 128]), op=ALU.mult),
            reads=[MS.v(512, 1024), cmask.v(0, 128)], writes=[sguw.v(l * 512, (l + 1) * 512)])

    wstate = {"i": 0, "converted": set(), "pinned": set(), "blk": 0, "br": 0}

    def getw(tile_id, pairs_fn, pin=False):
        si = wstate["i"] % NSLOT
        wstate["i"] += 1
        while si in wstate["pinned"]:
            si = wstate["i"] % NSLOT
            wstate["i"] += 1
        if pin:
            wstate["pinned"].add(si)
        slot = wring[si]
        full = slot.v(0, SLOT)
        if scratch and tile_id is not None and tile_id in wstate["converted"]:
            P.dma("sync", [(slot.t[:], wsc_d[tile_id])], wsem[si], reads=[("wsc", tile_id, tile_id + 1)], writes=[full])
        else:
            P.dma("gpsimd", pairs_fn(slot.t), wsem[si], writes=[full])
            if scratch and tile_id is not None:
                P.dma("sync", [(wsc_d[tile_id], slot.t[:])], wbsem[si], reads=[full], writes=[("wsc", tile_id, tile_id + 1)])
                wstate["converted"].add(tile_id)
        return slot

    brbuf = [Tn(nc, f"brb{i}", 2048, BF16) for i in range(2)]
    brsem = [P.new_sem(f"brs{i}") for i in range(2)]
    brwb = [P.new_sem(f"brw{i}") for i in range(2)]

    def getbr(l, tile_id, qd, b):
        k = wstate["br"] % 2
        wstate["br"] += 1
        buf = brbuf[k]
        full = buf.v(0, 2048)
        key = ("wscb", tile_id * 4 + b, tile_id * 4 + b + 1)
        img = wsc_d[tile_id][:, b * 2048:(b + 1) * 2048] if scratch else None
        if scratch and (tile_id, b) in wstate["converted"]:
            P.dma("sync", [(buf.t[:], img)], brsem[k], reads=[key], writes=[full])
        else:
            P.dma("gpsimd", [(buf.t[:, 0:2048].rearrange("p (c d) -> p c d", c=4),
                              brw_d[l, b][:, qd * 512:(qd + 1) * 512].rearrange("(c p) d -> p c d", p=128))], brsem[k], writes=[full])
            if scratch:
                P.dma("sync", [(img, buf.t[:])], brwb[k], reads=[full], writes=[key])
                wstate["converted"].add((tile_id, b))
        return buf

    def kc_view(dram2d, c0, ncols):
        return dram2d.rearrange("(kc p) f -> p kc f", p=128)[:, :, c0:c0 + ncols]

    def slot3(slot_t, a, b):
        return slot_t[:, 0:a * b].rearrange("p (a b) -> p a b", a=a)

    def prologue_ada():
        P.op("scalar", lambda e: e.activation(out=cond.t[:], in_=cpp.t[:, CPP_OFF["c"]:CPP_OFF["c"] + 16], func=AF.Silu),
             reads=[cc("c", 0, 16)], writes=[cond.v(0, 16)])
        crep = hT.v(0, 2048)
        P.op("vector", lambda e: e.tensor_copy(out=hT.t[:, 0:2048].rearrange("p (k m) -> p k m", k=16),
                                               in_=cond.t[:, 0:16].unsqueeze(2).broadcast_to([128, 16, 128])),
             reads=[cond.v(0, 16)], writes=[crep])
        for l in range(n_layers):
            for n in range(36):
                slot = getw(None, lambda st, l=l, n=n: [(slot3(st, 16, 512), kc_view(ada_w_d[l], n * 512, 512))])
                bk = bank()

                def fn(e, slot=slot, bk=bk):
                    ins = None
                    for kc in range(16):
                        ins = e.matmul(bk.t[:, 0:512], lhsT=hT.t[:, kc * 128:(kc + 1) * 128],
                                       rhs=slot.t[:, kc * 512:(kc + 1) * 512], start=(kc == 0), stop=(kc == 15))
                    return ins
                P.op("tensor", fn, reads=[slot.v(0, SLOT), crep], writes=[bk.v(0, 512)])
                tmp = MS.v(2048, 2560)
                P.op("vector", lambda e, bk=bk: e.tensor_tensor(
                    out=MS.t[:, 2048:2560].rearrange("p (j q) -> p j q", j=4),
                    in0=bk.t[:, 0:512].rearrange("p (j q) -> p j q", j=4),
                    in1=ident.t[:, 0:128].unsqueeze(1).broadcast_to([128, 4, 128]), op=ALU.mult),
                    reads=[bk.v(0, 512), ident.v(0, 128)], writes=[tmp])
                o = l * 144 + n * 4
                P.op("vector", lambda e, o=o: e.tensor_reduce(
                    out=adar.t[:, o:o + 4], in_=MS.t[:, 2048:2560].rearrange("p (j q) -> p j q", j=4),
                    axis=AX.X, op=ALU.add),
                    reads=[tmp], writes=[adar.v(o, o + 4)])
        nA = n_layers * 144
        P.op("vector", lambda e: e.tensor_tensor(out=adar.t[:, 0:nA], in0=adar.t[:, 0:nA],
                                                 in1=cpp.t[:, CPP_OFF["ada_b"]:CPP_OFF["ada_b"] + nA], op=ALU.add),
             reads=[adar.v(0, nA), cc("ada_b", 0, nA)], writes=[adar.v(0, nA)])
        for l in range(n_layers):
            for s in range(3):
                base = l * 144 + s * 48
                po = (l * 3 + s) * 16
                A0, S0, B0 = po, NL * 48 + po, 2 * NL * 48 + po
                P.op("vector", lambda e, base=base, po=po, A0=A0: e.scalar_tensor_tensor(
                    out=prm.t[:, A0:A0 + 16], in0=adar.t[:, base + 16:base + 32], scalar=1.0,
                    in1=cpp.t[:, CPP_OFF["pre_g"] + po:CPP_OFF["pre_g"] + po + 16], op0=ALU.add, op1=ALU.mult),
                    reads=[adar.v(base, base + 48), cc("pre_g", po, 16)], writes=[prm.v(A0, A0 + 16)])
                P.op("vector", lambda e, base=base, S0=S0: e.tensor_copy(out=prm.t[:, S0:S0 + 16], in_=adar.t[:, base:base + 16]),
                     reads=[adar.v(base, base + 48)], writes=[prm.v(S0, S0 + 16)])
                P.op("vector", lambda e, base=base, po=po, B0=B0, s=s: e.scalar_tensor_tensor(
                    out=prm.t[:, B0:B0 + 16], in0=adar.t[:, base + 32:base + 48], scalar=(1.0 if s == 1 else 0.5),
                    in1=cpp.t[:, CPP_OFF["post_g"] + po:CPP_OFF["post_g"] + po + 16], op0=ALU.mult, op1=ALU.mult),
                    reads=[adar.v(base, base + 48), cc("post_g", po, 16)], writes=[prm.v(B0, B0 + 16)])

    def prmv(kind, l, s, dc):
        o = kind * NL * 48 + (l * 3 + s) * 16 + dc
        return prm.v(o, o + 1)

    def mm_group(bk, T, pairs, reads, col0=0, fine=None):
        if fine is not None:
            n = len(pairs)
            tok = None
            for i, (l_, r_) in enumerate(pairs):
                def fn1(e, l_=l_, r_=r_, i=i, n=n):
                    return e.matmul(bk.t[:, col0:col0 + T], lhsT=l_, rhs=r_, start=(i == 0), stop=(i == n - 1))
                tok = P.op("tensor", fn1, reads=fine[i], writes=[bk.v(0, 512)])
            return tok

        def fn(e, pairs=pairs, bk=bk):
            ins = None
            n = len(pairs)
            for i, (l_, r_) in enumerate(pairs):
                ins = e.matmul(bk.t[:, col0:col0 + T], lhsT=l_, rhs=r_, start=(i == 0), stop=(i == n - 1))
            return ins
        return P.op("tensor", fn, reads=reads, writes=[bk.v(0, 512)])

    def stats_sum(src_chunks, T):
        bk = bank()
        mm_group(bk, T, [(ones.t[:, 0:128], s.ap) for s in src_chunks], reads=None,
                 fine=[[ones.v(0, 128), s] for s in src_chunks])
        return bk

    def rstd_from(bk, T, scale, eps, out_v, pre_sub=None):
        if pre_sub is None:
            P.op("scalar", lambda e: e.activation(out=out_v.ap, in_=bk.t[:, 0:T], func=AF.Ln, bias=float(eps), scale=float(scale)),
                 reads=[bk.v(0, 512)], writes=[out_v])
        else:
            P.op("vector", lambda e: e.scalar_tensor_tensor(out=out_v.ap, in0=bk.t[:, 0:T], scalar=float(scale), in1=pre_sub.ap,
                                                            op0=ALU.mult, op1=ALU.subtract),
                 reads=[bk.v(0, 512), pre_sub], writes=[out_v])
            P.op("scalar", lambda e: e.activation(out=out_v.ap, in_=out_v.ap, func=AF.Ln, bias=float(eps), scale=1.0),
                 reads=[out_v], writes=[out_v])
        P.op("scalar", lambda e: e.activation(out=out_v.ap, in_=out_v.ap, func=AF.Exp, scale=-0.5), reads=[out_v], writes=[out_v])

    def prenorm(l, s, T):
        sq = [hc(c, T) for c in range(16)]
        for c in range(16):
            P.op("scalar", lambda e, c=c: e.activation(out=sq[c].ap, in_=xc(c, T).ap, func=AF.Square),
                 reads=[xc(c, T)], writes=[sq[c]])
        bk = stats_sum(sq, T)
        rstd = tp(0, T)
        rstd_from(bk, T, 1.0 / D, RMS_EPS, rstd)
        for c in range(16):
            t = tp(1 + (c % 3), T)
            P.op("vector", lambda e, c=c, t=t: e.scalar_tensor_tensor(out=t.ap, in0=xc(c, T).ap, scalar=prmv(0, l, s, c).ap,
                                                                    in1=rstd.ap, op0=ALU.mult, op1=ALU.mult),
                 reads=[xc(c, T), prmv(0, l, s, c), rstd], writes=[t])
            P.op("scalar", lambda e, c=c, t=t: e.activation(out=nc_(c, T).ap, in_=t.ap, func=AF.Identity,
                                                            bias=prmv(1, l, s, c).ap, scale=1.0),
                 reads=[t, prmv(1, l, s, c)], writes=[nc_(c, T)])

    def postnorm(l, s, T):
        sq = [nc_(c, T) for c in range(16)]
        bk = stats_sum(sq, T)
        rstd = tp(0, T)
        rstd_from(bk, T, 1.0 / D, RMS_EPS, rstd)
        for g in range(4):
            yv = MS.v3(4 * g * TM, 4, TM, sub=(0, T))
            xv = xT.v3(4 * g * TM, 4, TM, sub=(0, T))
            P.op("vector", lambda e, yv=yv: e.tensor_tensor(out=yv.ap, in0=yv.ap, in1=rstd.ap.unsqueeze(1).broadcast_to([128, 4, T]), op=ALU.mult),
                 reads=[yv, rstd], writes=[yv])
            P.op("vector", lambda e, yv=yv, xv=xv: e.tensor_tensor(out=xv.ap, in0=xv.ap, in1=yv.ap, op=ALU.add),
                 reads=[yv, xv], writes=[xv])

    def evac_y(bk, dc, T, l, s):
        bv = prmv(2, l, s, dc)
        P.op("scalar", lambda e: e.activation(out=yc(dc, T).ap, in_=bk.t[:, 0:T], func=AF.Identity, scale=bv.ap),
             reads=[bk.v(0, 512), bv], writes=[yc(dc, T)])
        P.op("scalar", lambda e: e.activation(out=nc_(dc, T).ap, in_=bk.t[:, 0:T], func=AF.Square),
             reads=[bk.v(0, 512)], writes=[nc_(dc, T)])

    def tile_base(l):
        return l * TILES_PER_LAYER

    def ffn(l, s, f, T):
        prenorm(l, s, T)
        tb = tile_base(l) + (0 if f == 0 else 34)
        nall = [nc_(c, T) for c in range(16)]
        for g in range(22):
            slot = getw(tb + g, lambda st, g=g: [
                (slot3(st, 16, 512)[:, :, 0:256], kc_view(win_d[l, f], g * 256, 256)),
                (slot3(st, 16, 512)[:, :, 256:512], kc_view(win_d[l, f], FF + g * 256, 256))])
            for h in range(2):
                j = 2 * g + h
                bg, bu = bank(), bank()
                mm_group(bg, T, [(slot.t[:, kc * 512 + h * 128:kc * 512 + h * 128 + 128], nall[kc].ap) for kc in range(16)],
                         reads=[slot.v(0, SLOT)] + nall,
                         fine=([[slot.v(0, SLOT), nall[kc]] for kc in range(16)] if (g == 0 and h == 0) else None))
                mm_group(bu, T, [(slot.t[:, kc * 512 + 256 + h * 128:kc * 512 + 256 + h * 128 + 128], nall[kc].ap) for kc in range(16)],
                         reads=[slot.v(0, SLOT)] + nall)
                st_ = tp(4 + (j % 3), T)
                P.op("scalar", lambda e, bg=bg, st_=st_: e.activation(out=st_.ap, in_=bg.t[:, 0:T], func=AF.Silu),
                     reads=[bg.v(0, 512)], writes=[st_])
                P.op("vector", lambda e, bu=bu, st_=st_, j=j: e.tensor_tensor(out=hc(j, T).ap, in0=st_.ap, in1=bu.t[:, 0:T], op=ALU.mult),
                     reads=[bu.v(0, 512), st_], writes=[hc(j, T)])
        jt_sizes = (16, 16, 12)
        for q in range(4):
            bks = [bank() for _ in range(4)]
            j0 = 0
            for jt, nj in enumerate(jt_sizes):
                slot = getw(tb + 22 + q * 3 + jt, lambda st, q=q, j0=j0, nj=nj: [
                    (slot3(st, nj, 512), wout_d[l, f][j0 * 128:(j0 + nj) * 128, q * 512:(q + 1) * 512].rearrange("(j p) c -> p j c", p=128))])

                def fn(e, slot=slot, bks=bks, j0=j0, nj=nj):
                    ins = None
                    for jj in range(nj):
                        j = j0 + jj
                        for i in range(4):
                            ins = e.matmul(bks[i].t[:, 0:T], lhsT=slot.t[:, jj * 512 + i * 128:jj * 512 + i * 128 + 128],
                                           rhs=hT.t[:, j * TM:j * TM + T], start=(j == 0), stop=(j == 43))
                    return ins
                P.op("tensor", fn, reads=[slot.v(0, SLOT)] + [hc(j0 + jj, T) for jj in range(nj)], writes=[b.v(0, 512) for b in bks])
                j0 += nj
            for i in range(4):
                evac_y(bks[i], 4 * q + i, T, l, s)
        postnorm(l, s, T)

    A_BUF, ACC, PBUF, PT0, PT1, UB, VB, GB, CH = 0, 1656, 3192, 4792, 5192, 5592, 7128, 8664, 10200
    ACC2 = PBUF

    def mixer(l, T, blk):
        NT = T // 128
        prenorm(l, 1, T)
        tb = tile_base(l) + 68
        nall = [nc_(c, T) for c in range(16)]
        AW, PW, CW = 30 + TM, 16 + TM, 2 + TM

        def abuf(c, lo, hi):
            return MS.v(A_BUF + c * AW + lo, A_BUF + c * AW + hi)

        def pbuf(c, lo, hi):
            return MS.v(PBUF + c * PW + lo, PBUF + c * PW + hi)

        def chb(c, lo, hi):
            return MS.v(CH + c * CW + lo, CH + c * CW + hi)

        def acc(c):
            return MS.v(ACC + c * TM, ACC + c * TM + T)

        def ub(c):
            return MS.v(UB + c * TM, UB + c * TM + T)

        def vb(c):
            return MS.v(VB + c * TM, VB + c * TM + T)

        def gb(c):
            return MS.v(GB + c * TM, GB + c * TM + T)

        def cstv(c, o, n):
            b = (l * 4 + c) * 48 + o
            return cst.v(b, b + n)

        for c in range(4):
            P.op("vector", lambda e, c=c: e.tensor_copy(out=abuf(c, 0, 30).ap, in_=cstv(c, 0, 30).ap), reads=[cstv(c, 0, 30)], writes=[abuf(c, 0, 30)])
            P.op("vector", lambda e, c=c: e.tensor_copy(out=pbuf(c, 0, 16).ap, in_=cstv(c, 30, 16).ap), reads=[cstv(c, 30, 16)], writes=[pbuf(c, 0, 16)])
            P.op("vector", lambda e, c=c: e.tensor_copy(out=chb(c, 0, 2).ap, in_=cstv(c, 46, 2).ap), reads=[cstv(c, 46, 2)], writes=[chb(c, 0, 2)])

        def proj_tile(mt):
            slot = getw(tb + mt, lambda st, mt=mt: [(slot3(st, 16, 512), kc_view(mixw_d[l], mt * 512, 512))])
            for i in range(4):
                c = i
                bk = bank()
                mm_group(bk, T, [(slot.t[:, kc * 512 + i * 128:kc * 512 + i * 128 + 128], nall[kc].ap) for kc in range(16)],
                         reads=[slot.v(0, SLOT)] + nall,
                         fine=([[slot.v(0, SLOT), nall[kc]] for kc in range(16)] if (mt == 0 and i == 0) else None))
                src = bk.v(0, 512)
                if mt == 0:
                    d = abuf(c, 30, 30 + T)
                    P.op("scalar", lambda e, bk=bk, d=d: e.copy(out=d.ap, in_=bk.t[:, 0:T]), reads=[src], writes=[d])
                elif mt == 1:
                    t = tp(4 + (i % 3), T)
                    d = abuf(c, 30, 30 + T)
                    P.op("scalar", lambda e, bk=bk, t=t: e.activation(out=t.ap, in_=bk.t[:, 0:T], func=AF.Sigmoid), reads=[src], writes=[t])
                    P.op("vector", lambda e, t=t, d=d: e.tensor_tensor(out=d.ap, in0=d.ap, in1=t.ap, op=ALU.mult), reads=[t, d], writes=[d])
                elif mt == 2:
                    d = pbuf(c, 16, 16 + T)
                    P.op("scalar", lambda e, bk=bk, d=d: e.copy(out=d.ap, in_=bk.t[:, 0:T]), reads=[src], writes=[d])
                elif mt == 3:
                    d = ub(c)
                    P.op("scalar", lambda e, bk=bk, d=d: e.activation(out=d.ap, in_=bk.t[:, 0:T], func=AF.Gelu), reads=[src], writes=[d])
                elif mt == 4:
                    d = vb(c)
                    P.op("scalar", lambda e, bk=bk, d=d: e.activation(out=d.ap, in_=bk.t[:, 0:T], func=AF.Gelu), reads=[src], writes=[d])
                elif mt == 5:
                    d = gb(c)
                    P.op("scalar", lambda e, bk=bk, d=d: e.copy(out=d.ap, in_=bk.t[:, 0:T]), reads=[src], writes=[d])
                elif mt == 6:
                    d = chb(c, 2, 2 + T)
                    P.op("scalar", lambda e, bk=bk, d=d: e.copy(out=d.ap, in_=bk.t[:, 0:T]), reads=[src], writes=[d])
                else:
                    d = chb(c, 2, 2 + T)
                    P.op("vector", lambda e, bk=bk, d=d: e.tensor_tensor(out=d.ap, in0=d.ap, in1=bk.t[:, 0:T], op=ALU.mult), reads=[src, d], writes=[d])

        mk = cc("mask", 0)

        def mask_and_save(bufs_fn, so, sn):
            if blk == 0:
                for c in range(4):
                    d = bufs_fn(c, HALO)
                    P.op("vector", lambda e, d=d: e.tensor_scalar(out=d.ap, in0=d.ap, scalar1=mk.ap, scalar2=None, op0=ALU.mult),
                         reads=[d, mk], writes=[d])
            for c in range(4):
                src_ = bufs_fn(c, None)
                P.op("vector", lambda e, c=c, src_=src_: e.tensor_copy(out=cstv(c, so, sn).ap, in_=src_.ap), reads=[src_], writes=[cstv(c, so, sn)])

        for mt in (0, 1):
            proj_tile(mt)
        mask_and_save(lambda c, h: abuf(c, 30, 30 + h) if h else abuf(c, T, T + 30), 0, 30)

        def cw(c, k):
            return cc("conv_w", (l * 4 + c) * 31 + k)

        for c in range(4):
            P.op("vector", lambda e, c=c: e.tensor_scalar(out=acc(c).ap, in0=abuf(c, 0, T).ap, scalar1=cw(c, 0).ap,
                                                          scalar2=cc("conv_b", l * 4 + c).ap, op0=ALU.mult, op1=ALU.add),
                 reads=[abuf(c, 0, T), cw(c, 0), cc("conv_b", l * 4 + c)], writes=[acc(c)])
        for k in range(1, 31):
            for c in range(4):
                P.op("vector", lambda e, c=c, k=k: e.scalar_tensor_tensor(out=acc(c).ap, in0=abuf(c, k, k + T).ap, scalar=cw(c, k).ap,
                                                                          in1=acc(c).ap, op0=ALU.mult, op1=ALU.add),
                     reads=[abuf(c, k, k + T), cw(c, k), acc(c)], writes=[acc(c)])
        for mt in (6, 5, 2, 3, 4, 7):
            proj_tile(mt)
        mask_and_save(lambda c, h: pbuf(c, 16, 16 + h) if h else pbuf(c, T, T + 16), 30, 16)
        mask_and_save(lambda c, h: chb(c, 2, 2 + h) if h else chb(c, T, T + 2), 46, 2)

        def ln_stats(src, c0, mean, rstd):
            sb = [hc(c0 + c, T) for c in range(4)]
            sqb = [hc(c0 + 4 + c, T) for c in range(4)]
            for c in range(4):
                P.op("scalar", lambda e, c=c: e.copy(out=sb[c].ap, in_=src(c).ap), reads=[src(c)], writes=[sb[c]])
                P.op("scalar", lambda e, c=c: e.activation(out=sqb[c].ap, in_=src(c).ap, func=AF.Square), reads=[src(c)], writes=[sqb[c]])
            bm, bq = stats_sum(sb, T), stats_sum(sqb, T)
            msq = tp(1, T)
            P.op("vector", lambda e: e.tensor_scalar(out=mean.ap, in0=bm.t[:, 0:T], scalar1=1.0 / BW, scalar2=None, op0=ALU.mult),
                 reads=[bm.v(0, 512)], writes=[mean])
            P.op("vector", lambda e: e.tensor_tensor(out=msq.ap, in0=mean.ap, in1=mean.ap, op=ALU.mult), reads=[mean], writes=[msq])
            rstd_from(bq, T, 1.0 / BW, LN_EPS, rstd, pre_sub=msq)

        def ln_apply(src, mean, rstd, gname, bname, out_fn, act):
            for c in range(4):
                t = tp(4 + (c % 3), T)
                P.op("vector", lambda e, c=c, t=t: e.tensor_tensor(out=t.ap, in0=src(c).ap, in1=mean.ap, op=ALU.subtract),
                     reads=[src(c), mean], writes=[t])
                P.op("vector", lambda e, t=t: e.tensor_tensor(out=t.ap, in0=t.ap, in1=rstd.ap, op=ALU.mult), reads=[t, rstd], writes=[t])
                o = out_fn(c)
                P.op("scalar", lambda e, c=c, t=t, o=o: e.activation(out=o.ap, in_=t.ap, func=act, bias=cc(bname, l * 4 + c).ap,
                                                                     scale=cc(gname, l * 4 + c).ap),
                     reads=[t, cc(bname, l * 4 + c), cc(gname, l * 4 + c)], writes=[o])

        sgu_mean, sgu_rstd = tp(7, T), tp(3, T)
        conv_mean, conv_rstd = tp(0, T), tp(2, T)
        ln_stats(vb, 24, sgu_mean, sgu_rstd)
        ln_stats(acc, 16, conv_mean, conv_rstd)

        def pool_branch():
            pt = [MS.v(PT0, PT0 + PW), MS.v(PT1, PT1 + PW)]
            for g, w in enumerate((2, 4, 8, 16)):
                cur = MS.v(PBUF + g * PW, PBUF + (g + 1) * PW)
                curo = PBUF + g * PW
                sh = 1
                k = 0
                while sh < w:
                    dst = pt[k % 2]
                    dsto = PT0 if k % 2 == 0 else PT1
                    lo = 2 * sh - 1
                    P.op("vector", lambda e, curo=curo, dsto=dsto, sh=sh, lo=lo: e.tensor_tensor(
                        out=MS.t[:, dsto + lo:dsto + 16 + T], in0=MS.t[:, curo + lo:curo + 16 + T],
                        in1=MS.t[:, curo + lo - sh:curo + 16 + T - sh], op=ALU.add),
                        reads=[cur], writes=[dst])
                    cur, curo = dst, dsto
                    sh *= 2
                    k += 1
                pin = pbuf(g, 16, 16 + T)
                pooled = hc(32 + g, T)
                if blk == 0:
                    q = tp(1, T)
                    P.op("vector", lambda e, curo=curo, q=q, w=w: e.tensor_scalar(out=q.ap, in0=MS.t[:, curo + 16:curo + 16 + T], scalar1=1.0 / w,
                                                                                 scalar2=None, op0=ALU.mult), reads=[cur], writes=[q])
                    co = CPP_OFF["corr"] + g * 16
                    P.op("vector", lambda e, q=q, co=co: e.tensor_tensor(out=q.ap[:, HALO:HALO + 16], in0=q.ap[:, HALO:HALO + 16],
                                                                        in1=cpp.t[:, co:co + 16], op=ALU.mult),
                         reads=[q, cpp.v(co, co + 16)], writes=[q])
                    P.op("vector", lambda e, q=q, pin=pin, pooled=pooled: e.tensor_tensor(out=pooled.ap, in0=q.ap, in1=pin.ap, op=ALU.subtract),
                         reads=[q, pin], writes=[pooled])
                else:
                    P.op("vector", lambda e, curo=curo, pin=pin, pooled=pooled, w=w: e.scalar_tensor_tensor(
                        out=pooled.ap, in0=MS.t[:, curo + 16:curo + 16 + T], scalar=1.0 / w, in1=pin.ap, op0=ALU.mult, op1=ALU.subtract),
                        reads=[cur, pin], writes=[pooled])
                bk = bank()
                mm_group(bk, T, [(poolw.t[:, l * 512 + g * 128:l * 512 + (g + 1) * 128], pooled.ap)],
                         reads=[poolw.v(l * 512, (l + 1) * 512), pooled])
                o = hc(4 + g, T)
                P.op("scalar", lambda e, bk=bk, o=o, g=g: e.activation(out=o.ap, in_=bk.t[:, 0:T], func=AF.Identity,
                                                                       scale=cc("pool_scale", l * 4 + g).ap),
                     reads=[bk.v(0, 512), cc("pool_scale", l * 4 + g)], writes=[o])


        def sgu_branch():
            ln_apply(vb, sgu_mean, sgu_rstd, "sgu_ln_g", "sgu_ln_b", lambda c: hc(36 + c, T), AF.Identity)
            vtok0 = 40 * TM
            npair = NT * 4
            done = 0
            while done < npair:
                cnt = min(8, npair - done)
                srcs = []
                for idx in range(done, done + cnt):
                    n, g = divmod(idx, 4)
                    srcs.append(hT.v((36 + g) * TM + n * 128, (36 + g) * TM + (n + 1) * 128))

                def fn(e, srcs=srcs):
                    ins = None
                    for i, s_ in enumerate(srcs):
                        ins = e.transpose(out=psb.t[:, i * 128:(i + 1) * 128], in_=s_.ap, identity=identb.t[:, 0:128])
                    return ins
                P.op("tensor", fn, reads=srcs + [identb.v(0, 128)], writes=[psb.v(0, 1024)])
                dst = hT.v(vtok0 + done * 128, vtok0 + (done + cnt) * 128)
                P.op("scalar", lambda e, dst=dst, cnt=cnt: e.copy(out=dst.ap, in_=psb.t[:, 0:cnt * 128]), reads=[psb.v(0, 1024)], writes=[dst])
                done += cnt
            for g in range(4):
                bk = bank()

                def fn(e, bk=bk, g=g):
                    ins = None
                    for n in range(NT):
                        o = vtok0 + (n * 4 + g) * 128
                        ins = e.matmul(bk.t[:, n * 128:(n + 1) * 128], lhsT=hT.t[:, o:o + 128],
                                       rhs=sguw.t[:, l * 512 + g * 128:l * 512 + (g + 1) * 128], start=True, stop=True)
                    return ins
                P.op("tensor", fn, reads=[hT.v(vtok0, vtok0 + npair * 128), sguw.v(l * 512, (l + 1) * 512)], writes=[bk.v(0, 512)])
                t = tp(4 + (g % 3), T)
                bo = l * 512 + g * 128
                P.op("vector", lambda e, bk=bk, t=t, bo=bo: e.tensor_tensor(
                    out=t.ap.rearrange("p (n t) -> p n t", n=NT), in0=bk.t[:, 0:T].rearrange("p (n t) -> p n t", n=NT),
                    in1=bsb.t[:, bo:bo + 128].unsqueeze(1).broadcast_to([128, NT, 128]), op=ALU.add),
                    reads=[bk.v(0, 512), bsb.v(bo, bo + 128)], writes=[t])
                o = hc(8 + g, T)
                P.op("vector", lambda e, t=t, o=o, g=g: e.tensor_tensor(out=o.ap, in0=t.ap, in1=ub(g).ap, op=ALU.mult),
                     reads=[t, ub(g)], writes=[o])


        def conv_ln_branch():
            ln_apply(acc, conv_mean, conv_rstd, "conv_ln_g", "conv_ln_b", lambda c: hc(c, T), AF.Silu)


        def sconv_branch():
            def sw(c, k):
                return cc("sconv_w", (l * 4 + c) * 3 + k)

            for c in range(4):
                a2 = acc(c)
                P.op("vector", lambda e, c=c, a2=a2: e.tensor_scalar(out=a2.ap, in0=chb(c, 0, T).ap, scalar1=sw(c, 0).ap, scalar2=None, op0=ALU.mult),
                     reads=[chb(c, 0, T), sw(c, 0)], writes=[a2])
            for k in (1, 2):
                for c in range(4):
                    a2 = acc(c)
                    P.op("vector", lambda e, c=c, k=k, a2=a2: e.scalar_tensor_tensor(out=a2.ap, in0=chb(c, k, k + T).ap, scalar=sw(c, k).ap,
                                                                                    in1=a2.ap, op0=ALU.mult, op1=ALU.add),
                         reads=[chb(c, k, k + T), sw(c, k), a2], writes=[a2])
            for c in range(4):
                o = hc(12 + c, T)
                P.op("vector", lambda e, c=c, o=o: e.tensor_tensor(out=o.ap, in0=acc(c).ap, in1=gb(c).ap, op=ALU.mult),
                     reads=[acc(c), gb(c)], writes=[o])


        yb = [hc(c, T) for c in range(16)]
        BORDER = (1, 2, 0, 3)

        def macc(i):
            return MS.v(ACC2 + i * TM, ACC2 + i * TM + T)

        def gate_step(qd, b):
            first, last = (b == BORDER[0]), (b == BORDER[-1])
            brs = getbr(l, tb + 8 + qd * 5, qd, b)
            gs = getw(tb + 8 + qd * 5 + 1 + b, lambda st, qd=qd, b=b: [
                (slot3(st, 16, 512), kc_view(gatew_d[l, b], qd * 512, 512))])
            for i in range(4):
                dc = 4 * qd + i
                bg, by = bank(), bank()
                mm_group(bg, T, [(gs.t[:, kc * 512 + i * 128:kc * 512 + i * 128 + 128], nall[kc].ap) for kc in range(16)],
                         reads=[gs.v(0, SLOT)] + nall)
                mm_group(by, T, [(brs.t[:, c * 512 + i * 128:c * 512 + i * 128 + 128], yb[b * 4 + c].ap) for c in range(4)],
                         reads=[brs.v(0, 2048)] + yb[b * 4:b * 4 + 4])
                sg = tp(4 + (i % 2), T)
                gbv = cc("gate_b", (l * 4 + b) * 16 + dc)
                P.op("scalar", lambda e, bg=bg, sg=sg, gbv=gbv: e.activation(out=sg.ap, in_=bg.t[:, 0:T], func=AF.Sigmoid, bias=gbv.ap, scale=1.0),
                     reads=[bg.v(0, 512), gbv], writes=[sg])
                if first:
                    P.op("vector", lambda e, by=by, sg=sg, i=i: e.tensor_tensor(out=macc(i).ap, in0=sg.ap, in1=by.t[:, 0:T], op=ALU.mult),
                         reads=[by.v(0, 512), sg], writes=[macc(i)])
                else:
                    t = tp(6 if (i % 2) else 1, T)
                    P.op("vector", lambda e, by=by, sg=sg, t=t: e.tensor_tensor(out=t.ap, in0=sg.ap, in1=by.t[:, 0:T], op=ALU.mult),
                         reads=[by.v(0, 512), sg], writes=[t])
                    o = hc(16 + dc, T) if last else macc(i)
                    P.op("vector", lambda e, t=t, o=o, i=i: e.tensor_tensor(out=o.ap, in0=macc(i).ap, in1=t.ap, op=ALU.add),
                         reads=[t, macc(i)], writes=[o])

        if debug and l == 0 and blk == 0:
            pool_branch(); sgu_branch(); conv_ln_branch(); sconv_branch()
            P.dma("sync", [(dbg_d, hT.t[:, 0:16 * TM])], dsem["ld"], reads=[hT.v(0, 16 * TM)])
            for qd in range(4):
                for b in BORDER:
                    gate_step(qd, b)
        else:
            pool_branch()
            gate_step(0, 1)
            sgu_branch()
            gate_step(0, 2)
            conv_ln_branch()
            gate_step(0, 0)
            sconv_branch()
            gate_step(0, 3)
            for qd in range(1, 4):
                for b in BORDER:
                    gate_step(qd, b)

        mg = [hc(16 + c, T) for c in range(16)]
        if debug and l == 0 and blk == 0:
            P.dma("sync", [(dbg2_d, hT.t[:, 16 * TM:32 * TM])], dsem["ld"], reads=[hT.v(16 * TM, 32 * TM)])
        for qd in range(4):
            ws_ = getw(tb + 28 + qd, lambda st, qd=qd: [(slot3(st, 16, 512), kc_view(wo_d[l], qd * 512, 512))])
            for i in range(4):
                dc = 4 * qd + i
                bk = bank()
                mm_group(bk, T, [(ws_.t[:, kc * 512 + i * 128:kc * 512 + i * 128 + 128], mg[kc].ap) for kc in range(16)],
                         reads=[ws_.v(0, SLOT)] + mg)
                evac_y(bk, dc, T, l, 1)
        postnorm(l, 1, T)

    XS = 6144

    def load_x(eng, t_start, Tn_):
        P.dma(eng, [(MS.t[:, XS:XS + 16 * TM].rearrange("p (c t) -> p c t", c=16)[:, :, 0:Tn_], x_d[:, :, t_start:t_start + Tn_])],
              dsem["xin"], writes=[MS.v(XS, XS + 16 * TM)])

    load_x("sync", 0, blocks[0])
    prologue_ada()
    t0 = 0
    for blk, T in enumerate(blocks):
        NT = T // 128
        wstate["blk"] = blk
        if blk > 0 and blk % 2 == 0:
            P.new_epoch()
        for g in range(4):
            sv = MS.v3(XS + 4 * g * TM, 4, TM, sub=(0, T))
            dv = xT.v3(4 * g * TM, 4, TM, sub=(0, T))
            if g % 2:
                P.op("vector", lambda e, sv=sv, dv=dv: e.tensor_copy(out=dv.ap, in_=sv.ap), reads=[sv], writes=[dv])
            else:
                P.op("scalar", lambda e, sv=sv, dv=dv: e.copy(out=dv.ap, in_=sv.ap), reads=[sv], writes=[dv])
        prefetched = False
        for l in range(n_layers):
            if "ffn0" in stages:
                ffn(l, 0, 0, T)
            if "mix" in stages:
                mixer(l, T, blk)
            if l == n_layers - 1 and blk + 1 < len(blocks):
                load_x("scalar", t0 + T, blocks[blk + 1])
                prefetched = True
            if "ffn1" in stages:
                ffn(l, 2, 1, T)
        tlo = HALO if blk == 0 else 0
        for g in range(4):
            sv = xT.v3(4 * g * TM, 4, TM, sub=(0, T))
            dv = MS.v3(4 * g * TM, 4, TM, sub=(0, T))
            if g % 2:
                P.op("vector", lambda e, sv=sv, dv=dv: e.tensor_copy(out=dv.ap, in_=sv.ap), reads=[sv], writes=[dv])
            else:
                P.op("scalar", lambda e, sv=sv, dv=dv: e.copy(out=dv.ap, in_=sv.ap), reads=[sv], writes=[dv])
        if T > tlo:
            tok = P.dma("sync", [(out_d.rearrange("(c p) t -> p c t", p=128)[:, :, t0 - HALO + tlo:t0 - HALO + T],
                                  MS.t[:, 0:16 * TM].rearrange("p (c t) -> p c t", c=16)[:, :, tlo:T])],
                        dsem["xout"], reads=[MS.v(0, 16 * TM)])
            last_tok = tok
        t0 += T
    P.wait_on("sync", last_tok)
    P.emit()
    return nc


BLOCKS_FULL = [384, 384, 384, 384, 384, 256]


def _host_inputs(inp, core, blocks):
    ntok = sum(blocks)
    x = np.asarray(inp["x"], np.float32)[0]
    g0 = core * TOK_PER_CORE - HALO
    xs = np.zeros((ntok, D), np.float32)
    lo = max(g0, 0)
    xs[lo - g0:, :] = x[lo:g0 + ntok]
    f = lambda k: np.ascontiguousarray(np.asarray(inp[k], np.float32))
    m = {
        "x": np.ascontiguousarray(xs.T.reshape(16, 128, ntok).transpose(1, 0, 2)),
        "cpp": _build_cpp(inp, core),
        "bsb": np.ascontiguousarray(np.broadcast_to(np.asarray(inp["sgu_b_s"], np.float32).reshape(1, NL * 512), (128, NL * 512))),
        "ident": np.eye(128, dtype=np.float32),
        "cmask": np.triu(np.ones((128, 128), np.float32)),
        "ada_w": f("ada_w"), "ffn_w_in": f("ffn_w_in"), "ffn_w_out": f("ffn_w_out"), "mix_w_in": f("mix_w_in"),
        "gate_w": f("gate_w"), "branch_w_out": f("branch_w_out"), "w_o": f("w_o"), "pool_group_w": f("pool_group_w"),
        "sgu_w_sT": np.ascontiguousarray(np.asarray(inp["sgu_w_s"], np.float32).transpose(0, 1, 3, 2)),
    }
    return m


def kernel(**inp):
    blocks = BLOCKS_FULL
    nc = build(blocks)
    in_maps = [_host_inputs(inp, c, blocks) for c in range(NCORES)]
    res = run_bass_kernel_spmd(nc, in_maps, core_ids=list(range(NCORES)))
    out = np.concatenate([np.asarray(r["out"], np.float32).T for r in res.results], axis=0)
    return np.ascontiguousarray(out).reshape(1, NCORES * TOK_PER_CORE, D)
```
